# Optimizing a Trainium2 kernel written in Bass

```python
import math
import jax, jax.numpy as jnp
from jax import lax
import numpy as np

D_MODEL = 1024
BATCH = 16
SEQ = 256
DEPTH = 4
DEC_BATCH = 2
DEC_SEQ = 1024
PAST_LEN = 256

GRID_W = 64
CHUNK = 128
N_GROUPS_A = 4
D_A = D_MODEL
N_HEADS = 8
HEAD_DIM = D_MODEL // (2 * N_HEADS)
V_DIM = 2 * HEAD_DIM
ATTN_W = N_HEADS * V_DIM
D_FF = ((8 * D_MODEL // 3 + 127) // 128) * 128
IN_W = 2 * D_A + 3 * ATTN_W + 2 * D_MODEL
N_MOD = 9
ROPE_THETA = 10000.0
Q_BLOCK = 128
EPS = 1e-6

kernel_name = 'hybrid_dit_gmlp_diffattn_step'


def rmsnorm(x, g):
    xf = x.astype(jnp.float32)
    r = lax.rsqrt(jnp.mean(xf * xf, axis=-1, keepdims=True) + EPS)
    return (xf * r).astype(x.dtype) * g


def swiglu(h, w13, w2):
    gate, up = jnp.split(h @ w13, 2, axis=-1)
    return (jax.nn.silu(gate) * up) @ w2


def spatial_gating(z, v_gain, w_s, b_s):
    u, v = jnp.split(z, 2, axis=-1)
    v = rmsnorm(v, v_gain)
    B, N, _ = v.shape
    vc = v.reshape(B, N // CHUNK, CHUNK, N_GROUPS_A, D_A // N_GROUPS_A)
    mixed = jnp.einsum('gpq,bnqgc->bnpgc', w_s, vc) + jnp.swapaxes(b_s, 0, 1)[:, :, None]
    return u * mixed.reshape(B, N, D_A)


def rope_half(x, ang):
    cos = jnp.cos(ang)[None, :, None, None, :].astype(x.dtype)
    sin = jnp.sin(ang)[None, :, None, None, :].astype(x.dtype)
    x1, x2 = jnp.split(x, 2, axis=-1)
    return jnp.concatenate([x1 * cos - x2 * sin, x2 * cos + x1 * sin], axis=-1)


def axial_rope(x):
    n = x.shape[1]
    n_rows = n // GRID_W
    row = jnp.repeat(jnp.arange(n_rows, dtype=jnp.float32), GRID_W)
    col = jnp.tile(jnp.arange(GRID_W, dtype=jnp.float32), n_rows)
    half = HEAD_DIM // 2
    freqs = ROPE_THETA ** (-jnp.arange(0, half, 2, dtype=jnp.float32) / half)
    xr = rope_half(x[..., :half], row[:, None] * freqs[None, :])
    xc = rope_half(x[..., half:], col[:, None] * freqs[None, :])
    return jnp.concatenate([xr, xc], axis=-1)


def diff_attention(q, k, v, lam):
    B, Nq = q.shape[0], q.shape[1]
    nb = Nq // Q_BLOCK
    qb = jnp.swapaxes(q.reshape(B, nb, Q_BLOCK, N_HEADS, 2, HEAD_DIM), 0, 1)
    scale = HEAD_DIM ** -0.5

    def block(qi):
        s = jnp.einsum('bqhid,bkhid->bihqk', qi, k).astype(jnp.float32) * scale
        p = jax.nn.softmax(s, axis=-1)
        a = p[:, 0] - lam * p[:, 1]
        return jnp.einsum('bhqk,bkhe->bqhe', a.astype(v.dtype), v)

    out = lax.map(block, qb)
    return jnp.swapaxes(out, 0, 1).reshape(B, Nq, N_HEADS, V_DIM)


def setup_inputs(seed: int = 0) -> dict:
    key = jax.random.key(seed)
    ks = jax.random.split(key, 32)
    f32 = jnp.float32

    def nrm(k, shape, scale):
        return jax.random.normal(k, shape, f32) * scale

    def gain(k, shape):
        return 1.0 + 0.01 * jax.random.normal(k, shape, f32)

    return {
        'x_prompt': nrm(ks[0], (BATCH, SEQ, D_MODEL), 1.0),
        'x_sample': nrm(ks[1], (DEC_BATCH, DEC_SEQ, D_MODEL), 1.0),
        'cache_k': nrm(ks[2], (DEC_BATCH, DEPTH, PAST_LEN, N_HEADS, 2 * HEAD_DIM), 1.0),
        'cache_v': nrm(ks[3], (DEC_BATCH, DEPTH, PAST_LEN, N_HEADS, V_DIM), 1.0),
        'c': nrm(ks[4], (DEC_BATCH, D_MODEL), 1.0),
        'c_ctx': nrm(ks[5], (D_MODEL,), 1.0),
        'w_mod': nrm(ks[6], (DEPTH, D_MODEL, N_MOD * D_MODEL), 0.5 * D_MODEL ** -0.5),
        'b_mod': nrm(ks[7], (DEPTH, N_MOD * D_MODEL), 0.01),
        'g_norm': gain(ks[8], (DEPTH, 3, D_MODEL)),
        'ffn1_w13': nrm(ks[9], (DEPTH, D_MODEL, 2 * D_FF), D_MODEL ** -0.5),
        'ffn1_w2': nrm(ks[10], (DEPTH, D_FF, D_MODEL), D_FF ** -0.5),
        'w_in': nrm(ks[11], (DEPTH, D_MODEL, IN_W), D_MODEL ** -0.5),
        'sgu_gain': gain(ks[12], (DEPTH, D_A)),
        'w_spatial': nrm(ks[13], (DEPTH, N_GROUPS_A, CHUNK, CHUNK), CHUNK ** -0.5),
        'b_spatial': gain(ks[14], (DEPTH, N_GROUPS_A, CHUNK)),
        'lam': nrm(ks[15], (DEPTH, 4, HEAD_DIM), 0.1),
        'subln_gain': gain(ks[16], (DEPTH, V_DIM)),
        'w_branch_a': nrm(ks[17], (DEPTH, D_A, D_MODEL), D_A ** -0.5),
        'w_branch_b': nrm(ks[18], (DEPTH, ATTN_W, D_MODEL), ATTN_W ** -0.5),
        'w_out': nrm(ks[19], (DEPTH, D_MODEL, D_MODEL), D_MODEL ** -0.5),
        'ffn2_w13': nrm(ks[20], (DEPTH, D_MODEL, 2 * D_FF), D_MODEL ** -0.5),
        'ffn2_w2': nrm(ks[21], (DEPTH, D_FF, D_MODEL), D_FF ** -0.5),
        'g_final': gain(ks[22], (D_MODEL,)),
    }


def reference(x_prompt, x_sample, cache_k, cache_v, c, c_ctx, w_mod, b_mod, g_norm,
              ffn1_w13, ffn1_w2, w_in, sgu_gain, w_spatial, b_spatial, lam, subln_gain,
              w_branch_a, w_branch_b, w_out, ffn2_w13, ffn2_w2, g_final):

    def mixer(h, l, ctx_kv, use_rope):
        B, N, _ = h.shape
        proj = h @ w_in[l]
        z, q, k, v, gates = jnp.split(
            proj, [2 * D_A, 2 * D_A + ATTN_W, 2 * D_A + 2 * ATTN_W, 2 * D_A + 3 * ATTN_W], axis=-1)
        a = spatial_gating(jax.nn.gelu(z), sgu_gain[l], w_spatial[l], b_spatial[l])
        q = q.reshape(B, N, N_HEADS, 2, HEAD_DIM)
        k = k.reshape(B, N, N_HEADS, 2, HEAD_DIM)
        v = v.reshape(B, N, N_HEADS, V_DIM)
        k_state = k.reshape(B, N, N_HEADS, 2 * HEAD_DIM)
        v_state = v
        if use_rope:
            q = axial_rope(q)
            k = axial_rope(k)
        if ctx_kv is not None:
            ck, cv = ctx_kv
            k = jnp.concatenate([ck.reshape(B, ck.shape[1], N_HEADS, 2, HEAD_DIM), k], axis=1)
            v = jnp.concatenate([cv, v], axis=1)
        lam_init = 0.8 - 0.6 * math.exp(-0.3 * l)
        lp = lam[l].astype(jnp.float32)
        lam_full = jnp.exp(jnp.sum(lp[0] * lp[1])) - jnp.exp(jnp.sum(lp[2] * lp[3])) + lam_init
        o = diff_attention(q, k, v, lam_full)
        o = rmsnorm(o, subln_gain[l]) * (1.0 - lam_init)
        b = o.reshape(B, N, ATTN_W)
        g_a, g_b = jnp.split(gates, 2, axis=-1)
        merged = jax.nn.sigmoid(g_a) * (a @ w_branch_a[l]) + jax.nn.sigmoid(g_b) * (b @ w_branch_b[l])
        return merged @ w_out[l], k_state, v_state

    def run_layer(x, cond, l, ctx_kv, use_rope):
        mod = jax.nn.silu(cond) @ w_mod[l] + b_mod[l]
        sh1, sc1, gt1, sh2, sc2, gt2, sh3, sc3, gt3 = jnp.split(mod, N_MOD, axis=-1)
        h = rmsnorm(x, g_norm[l, 0]) * (1.0 + sc1) + sh1
        x = x + 0.5 * gt1 * swiglu(h, ffn1_w13[l], ffn1_w2[l])
        h = rmsnorm(x, g_norm[l, 1]) * (1.0 + sc2) + sh2
        m, k_state, v_state = mixer(h, l, ctx_kv, use_rope)
        x = x + gt2 * m
        h = rmsnorm(x, g_norm[l, 2]) * (1.0 + sc3) + sh3
        x = x + 0.5 * gt3 * swiglu(h, ffn2_w13[l], ffn2_w2[l])
        return x, k_state, v_state

    xp = x_prompt
    k_list, v_list = [], []
    for l in range(DEPTH):
        xp, k_l, v_l = run_layer(xp, c_ctx, l, None, False)
        k_list.append(k_l)
        v_list.append(v_l)
    y_prompt = rmsnorm(xp, g_final)
    new_cache_k = jnp.stack(k_list, axis=1)
    new_cache_v = jnp.stack(v_list, axis=1)

    xs = x_sample
    cond = c[:, None, :]
    for l in range(DEPTH):
        xs, _, _ = run_layer(xs, cond, l, (cache_k[:, l], cache_v[:, l]), True)
    y_sample = rmsnorm(xs, g_final)

    return (y_prompt, y_sample, new_cache_k, new_cache_v)
```

```python
import contextlib
import math
import numpy as np
import concourse.bass as bass
import concourse.mybir as mybir
from concourse.bass_utils import run_bass_kernel_spmd

F32 = mybir.dt.float32
BF16 = mybir.dt.bfloat16
AF = mybir.ActivationFunctionType
ALU = mybir.AluOpType
AX = mybir.AxisListType

NCORES = 8
D = 1024
DFF = 2816
T = 768
EPS = 1e-6
EPOCH = 4000
ENGS = ("pe", "act", "dve", "pool", "sp")
GROUPS = [(0, 512, 0), (512, 640, 1), (640, 768, 2)]
CH = [(0, 512), (512, 768)]
NSLOT = 4
PREF = 2
CL = 25
NCONST = 4 * CL + 8


def lam_init(l):
    return 0.8 - 0.6 * math.exp(-0.3 * l)


class Buf:
    __slots__ = ("name", "w", "r", "dkey", "dcnt", "over")

    def __init__(self, name):
        self.name = name
        self.w = {}
        self.r = {}
        self.dkey = None
        self.dcnt = 0
        self.over = []


class Prog:
    def __init__(self, nc, stack):
        self.nc = nc
        self.stack = stack
        self.streams = {e: [] for e in ENGS}
        self.count = {e: 0 for e in ENGS}
        self.seen = {e: {} for e in ENGS}
        self.sems = {}

    def sem(self, key):
        s = self.sems.get(key)
        if s is None:
            s = self.stack.enter_context(self.nc.semaphore("s_%s" % "_".join(str(k) for k in key)))
            self.sems[key] = s
        return s

    def sb(self, name, shape, dt):
        return self.stack.enter_context(self.nc.sbuf_tensor("sb_" + name, list(shape), dt))

    def ps(self, name, shape, dt=F32):
        return self.stack.enter_context(self.nc.psum_tensor("ps_" + name, list(shape), dt))

    def _deps(self, e, reads, writes):
        waits = {}

        def need(k, v):
            if self.seen[e].get(k, 0) < v:
                self.seen[e][k] = v
                waits[k] = max(waits.get(k, 0), v)

        for b in reads:
            for k, v in b.w.items():
                if e == "pe" and k[0] == "pe":
                    continue
                need(k, v)
        for b0 in writes:
            for b in [b0] + b0.over:
                for k, v in b.w.items():
                    if k[0] == e:
                        continue
                    need(k, v)
                for k, v in b.r.items():
                    if k[0] == e:
                        continue
                    need(k, v)
        for k, v in waits.items():
            self.streams[e].append(("wait", k, v))

    def _mark(self, tok, reads, writes):
        k, v = tok
        for b in writes:
            if b.w.get(k, 0) < v:
                b.w[k] = v
        for b in reads:
            if b.r.get(k, 0) < v:
                b.r[k] = v

    def op(self, e, fn, reads=(), writes=()):
        self._deps(e, reads, writes)
        self.count[e] += 1
        c = self.count[e]
        ep = (c - 1) // EPOCH
        tok = ((e, ep), c - ep * EPOCH)
        self.sem(tok[0])
        self.streams[e].append(("ins", fn, tok[0], 1))
        self._mark(tok, reads, writes)
        return tok

    def dma(self, q, fn, reads=(), writes=(), owner=None, inc=16):
        self._deps(q, reads, writes)
        b = owner
        if b.dkey is None or b.dcnt + inc > 3500:
            b.dkey = ("d", b.name, 0 if b.dkey is None else b.dkey[2] + 1)
            b.dcnt = 0
            self.sem(b.dkey)
        b.dcnt += inc
        tok = (b.dkey, b.dcnt)
        self.streams[q].append(("ins", fn, b.dkey, inc))
        self._mark(tok, reads, writes)
        return tok

    def wait_all(self, e, bufs):
        waits = {}
        for b in bufs:
            for d in (b.w, b.r):
                for k, v in d.items():
                    if self.seen[e].get(k, 0) < v:
                        self.seen[e][k] = v
                        waits[k] = max(waits.get(k, 0), v)
        for k, v in waits.items():
            self.streams[e].append(("wait", k, v))

    def emit(self):
        nc = self.nc
        P = self

        def run(eng, items):
            for it in items:
                if it[0] == "wait":
                    eng.wait_ge(P.sems[it[1]], it[2])
                else:
                    it[1](eng).then_inc(P.sems[it[2]], it[3])

        with nc.Block() as block:
            @block.sync
            def _(eng):
                run(eng, P.streams["sp"])

            @block.tensor
            def _(eng):
                run(eng, P.streams["pe"])

            @block.scalar
            def _(eng):
                run(eng, P.streams["act"])

            @block.vector
            def _(eng):
                run(eng, P.streams["dve"])

            @block.gpsimd
            def _(eng):
                run(eng, P.streams["pool"])


def build(NL=4):
    nc = bass.Bass("TRN2", target_bir_lowering=False)

    def din(name, shape, dt=F32):
        return nc.dram_tensor(name, list(shape), dt, kind="ExternalInput").ap()

    def dout(name, shape, dt=F32):
        return nc.dram_tensor(name, list(shape), dt, kind="ExternalOutput").ap()

    xT_d = din("xT", [D, T])
    condT_d = din("condT", [128, 24])
    consts_d = din("consts", [128, NCONST])
    sgu_d = din("sgu", [4, 1024])
    bsp_d = din("bsp", [4, 512])
    lam_d = din("lam", [1, 1024])
    wsT_d = din("wsT", [128, 2048])
    rope_d = din("rope", [128, 256])
    perm_d = din("perm", [128, 128])
    wmod_r = din("wmod_r", [1024, 4608])
    bmod_r = din("bmod_r", [128, 36])
    f1_w13 = din("ffn1_w13", [4, 1024, 2 * DFF])
    f1_w2 = din("ffn1_w2", [4, DFF, 1024])
    w_in = din("w_in", [4, 1024, 7168])
    w_ba = din("w_branch_a", [4, 1024, 1024])
    w_bb = din("w_branch_b", [4, 1024, 1024])
    w_o = din("w_out", [4, 1024, 1024])
    f2_w13 = din("ffn2_w13", [4, 1024, 2 * DFF])
    f2_w2 = din("ffn2_w2", [4, DFF, 1024])
    ckT_d = din("ckT", [4, 2, 1024, 256])
    cv_d = din("cv", [4, 2, 256, 1024])
    yT_d = dout("yT", [D, T])
    nkT_d = dout("nkT", [4, 1024, 512])
    nv_d = dout("nv", [4, 512, 1024])
    agk_in = nc.dram_tensor("agk_in", [1024, 256], BF16).ap()
    agk_out = nc.dram_tensor("agk_out", [NCORES * 1024, 256], BF16).ap()
    agv_in = nc.dram_tensor("agv_in", [256, 1024], BF16).ap()
    agv_out = nc.dram_tensor("agv_out", [NCORES * 256, 1024], BF16).ap()
    agm_in = nc.dram_tensor("agm_in", [128, 108], F32).ap()
    agm_out = nc.dram_tensor("agm_out", [NCORES * 128, 108], F32).ap()

    with contextlib.ExitStack() as stack:
        P = Prog(nc, stack)
        xT = P.sb("xT", [128, 8, T], F32)
        bxT = [Buf("xT%d" % i) for i in range(8)]
        hT = P.sb("hT", [128, 8, T], BF16); bhT = Buf("hT")
        R = P.sb("R", [128, 45568], BF16)
        gT = R[:, 0:16896].rearrange("p (k t) -> p k t", k=22); bG = Buf("G")
        uaT = R[:, 0:6144].rearrange("p (k t) -> p k t", k=8)
        vn = R[:, 6144:12288].rearrange("p (k c) -> p k c", k=6)
        qT = R[:, 12288:18432].rearrange("p (k t) -> p k t", k=8)
        mT = qT
        bT = R[:, 18432:24576].rearrange("p (k t) -> p k t", k=8); bbT = Buf("bT")
        sq = bT
        kp = R[:, 24576:28672].rearrange("p (k t) -> p k t", k=8); bkp = Buf("kp")
        Vb = R[:, 28672:32768].rearrange("p (k c) -> p k c", k=4); bVb = Buf("Vb")
        ET = [R[:, 32768 + i * 1280: 32768 + (i + 1) * 1280] for i in range(2)]
        bET = [Buf("ET%d" % i) for i in range(2)]
        K_s = R[:, 35328:40448].rearrange("p (k t) -> p k t", k=4); bKs = Buf("Ks")
        V_s = R[:, 40448:45568].rearrange("p (k c) -> p k c", k=10); bVs = Buf("Vs")
        bua = Buf("ua"); bvn = Buf("vn"); bq = Buf("q")
        ET2 = [R[:, 6144 + i * 1280: 6144 + (i + 1) * 1280] for i in range(2)]
        bET2 = [Buf("ET2_%d" % i) for i in range(2)]
        oo_b = [R[:, 8704 + i * 512: 8704 + (i + 1) * 512].bitcast(F32) for i in range(2)]
        boo = [Buf("oo%d" % i) for i in range(2)]
        osq_b = [R[:, 9728 + i * 256: 9728 + (i + 1) * 256] for i in range(2)]
        bosq_b = [Buf("osq%d" % i) for i in range(2)]
        vn_alias = bET2 + boo + bosq_b
        bG.over = [bua, bvn, bq] + vn_alias
        bua.over = [bG]; bq.over = [bG]
        bvn.over = [bG] + vn_alias
        for b_ in vn_alias:
            b_.over = [bvn, bG]
        ring = [P.sb("ring%d" % i, [128, 4096], BF16) for i in range(NSLOT)]
        bring = [Buf("ring%d" % i) for i in range(NSLOT)]
        vstg = [P.sb("vstg%d" % i, [128, 1024], F32) for i in range(2)]
        bvstg = [Buf("vstg%d" % i) for i in range(2)]
        kstg = [P.sb("kstg%d" % i, [128, 512], F32) for i in range(2)]
        bkstg = [Buf("kstg%d" % i) for i in range(2)]
        ksmp = P.sb("ksmp", [128, 8, 256], BF16); bksmp = Buf("ksmp")
        vsmp = P.sb("vsmp", [128, 2, 1024], BF16); bvsmp = Buf("vsmp")
        rstd = P.sb("rstd", [128, T], F32); brstd = Buf("rstd")
        nt = [vstg[i][:, 0:T] for i in range(2)]
        bnt = bvstg
        tmp = [P.sb("tmp%d" % i, [128, 512], F32) for i in range(4)]
        btmp = [Buf("tmp%d" % i) for i in range(4)]
        consts = P.sb("consts", [128, NCONST], F32); bconsts = Buf("consts")
        condT = P.sb("condT", [128, 24], F32); bcond = Buf("condT")
        scT = P.sb("scT", [128, 24], BF16); bsc = Buf("scT")
        modall = P.sb("modall", [128, 8 * 108], F32); bmodall = Buf("modall")
        modpart = P.sb("modpart", [128, 108], F32); bmodpart = Buf("modpart")
        bmr = P.sb("bmr", [128, 36], F32); bbmr = Buf("bmr")
        Aj = P.sb("Aj", [128, 4 * 72], F32)
        Gj = P.sb("Gj", [128, 4 * 72], F32)
        bder1 = Buf("der")
        bmod = [bmodall, bmodall]
        bder = [bder1, bder1]
        bagm_in = Buf("agm_in"); bagm_out = Buf("agm_out")
        sgu_bc = P.sb("sgu_bc", [128, 1024], F32); bsgu = Buf("sgu")
        bsp_bc = P.sb("bsp_bc", [128, 512], F32); bbsp = Buf("bsp")
        lam_bc = vstg[0]; blam = bvstg[0]
        lsm = P.sb("lsm", [128, 32], F32); blsm = Buf("lsm")
        wsT = P.sb("wsT", [128, 2048], BF16); bws = Buf("wsT")
        rope = P.sb("rope", [128, 256], F32); brope = Buf("rope")
        perm = P.sb("perm", [128, 128], F32); bperm = Buf("perm")
        ones = P.sb("ones", [128, 128], BF16); bones = Buf("ones")
        ssq = P.sb("ssq", [128, 16], F32); bssq = Buf("ssq")
        B = [P.ps("bank%d" % i, [128, 512], F32) for i in range(8)]
        bB = [Buf("bank%d" % i) for i in range(8)]
        bsetup = Buf("setup")
        bout_y = Buf("out_y"); bout_k = Buf("out_k"); bout_v = Buf("out_v")
        bagk_in = Buf("agk_in"); bagk_out = Buf("agk_out"); bagv_in = Buf("agv_in"); bagv_out = Buf("agv_out")
        eps_ap = lsm[:, 24:25]

        def mm(out, lhsT, rhs, start, stop, reads, writes):
            P.op("pe", lambda e: e.matmul(out, lhsT=lhsT, rhs=rhs, start=start, stop=stop), reads=reads, writes=writes)

        def act(out, in_, func, reads, writes, scale=1.0, bias=None):
            if bias is None:
                P.op("act", lambda e: e.activation(out=out, in_=in_, func=func, scale=scale), reads=reads, writes=writes)
            else:
                P.op("act", lambda e: e.activation(out=out, in_=in_, func=func, scale=scale, bias=bias), reads=reads, writes=writes)

        def tt(out, in0, in1, op, reads, writes):
            P.op("dve", lambda e: e.tensor_tensor(out=out, in0=in0, in1=in1, op=op), reads=reads, writes=writes)

        def stt(out, in0, scalar, in1, op0, op1, reads, writes, accum_out=None):
            if accum_out is None:
                P.op("dve", lambda e: e.scalar_tensor_tensor(out=out, in0=in0, scalar=scalar, in1=in1, op0=op0, op1=op1),
                     reads=reads, writes=writes)
            else:
                P.op("dve", lambda e: e.scalar_tensor_tensor(out=out, in0=in0, scalar=scalar, in1=in1, op0=op0, op1=op1,
                                                              accum_out=accum_out), reads=reads, writes=writes)

        def ts(out, in0, s1, s2, op0, op1, reads, writes):
            if s2 is None:
                P.op("dve", lambda e: e.tensor_scalar(out=out, in0=in0, scalar1=s1, scalar2=None, op0=op0), reads=reads, writes=writes)
            else:
                P.op("dve", lambda e: e.tensor_scalar(out=out, in0=in0, scalar1=s1, scalar2=s2, op0=op0, op1=op1),
                     reads=reads, writes=writes)

        def vcopy(out, in_, reads, writes):
            P.op("dve", lambda e: e.tensor_copy(out=out, in_=in_), reads=reads, writes=writes)

        def acopy(out, in_, reads, writes):
            P.op("act", lambda e: e.copy(out=out, in_=in_), reads=reads, writes=writes)

        def ld(q, out, in_, writes, owner, reads=()):
            P.dma(q, lambda e: e.dma_start(out=out, in_=in_), reads=reads, writes=writes, owner=owner)

        def w_rows(ap2d):
            return ap2d.rearrange("(k p) c -> p k c", p=128)

        def panel_parts(kind, l, i):
            if kind == "modr":
                return 8, 512, [(wmod_r[:, i * 512:(i + 1) * 512], 0)]
            if kind in ("f1a", "f2a"):
                w = f1_w13 if kind == "f1a" else f2_w13
                return 8, 512, [(w[l, :, i * 256:(i + 1) * 256], 0), (w[l, :, DFF + i * 256: DFF + (i + 1) * 256], 256)]
            if kind in ("f1b", "f2b"):
                w = f1_w2 if kind == "f1b" else f2_w2
                return 22, 128, [(w[l, :, i * 128:(i + 1) * 128], 0)]
            if kind in ("mu", "mv", "mq", "mk", "mva"):
                base = {"mu": 0, "mv": 1024, "mq": 2048, "mk": 3072, "mva": 4096}[kind]
                return 8, 512, [(w_in[l, :, base + i * 512: base + (i + 1) * 512], 0)]
            if kind == "mb":
                return 8, 512, [(w_in[l, :, 5120 + i * 128: 5120 + (i + 1) * 128], 0),
                                (w_in[l, :, 6144 + i * 128: 6144 + (i + 1) * 128], 128),
                                (w_ba[l, :, i * 128:(i + 1) * 128], 256),
                                (w_bb[l, :, i * 128:(i + 1) * 128], 384)]
            if kind == "mo":
                return 8, 512, [(w_o[l, :, i * 512:(i + 1) * 512], 0)]
            raise ValueError(kind)

        def layer_units(l):
            U = []
            for k1, k2 in (("f1a", "f1b"),):
                U += [[(k1, l, i)] for i in range(11)] + [[(k2, l, i)] for i in range(8)]
            U += [[("mu", l, 0)], [("mu", l, 1)], [("mv", l, 0), ("mv", l, 1)], [("mq", l, 0)], [("mq", l, 1)],
                  [("mk", l, 0)], [("mk", l, 1)], [("mva", l, 0), ("mva", l, 1)]]
            U += [[("mb", l, i)] for i in range(8)]
            U += [[("mo", l, 0)], [("mo", l, 1)]]
            U += [[("f2a", l, i)] for i in range(11)] + [[("f2b", l, i)] for i in range(8)]
            return U

        plist = [("modr", 0, i) for i in range(9)]
        for l in range(NL):
            for u in layer_units(l):
                plist += u
        pstate = {"next_use": 0, "next_load": 0}

        def issue_loads(upto):
            while pstate["next_load"] <= min(upto, len(plist) - 1):
                n = pstate["next_load"]
                kind, l, i = plist[n]
                kt, W, parts = panel_parts(kind, l, i)
                slot = ring[n % NSLOT]
                sv = slot[:, 0:kt * W].rearrange("p (k c) -> p k c", k=kt)
                for (src, off) in parts:
                    nco = src.shape[-1]
                    ld("pool", sv[:, :, off:off + nco], w_rows(src), [bring[n % NSLOT]], bring[n % NSLOT])
                pstate["next_load"] += 1

        def pull(kind, l, i):
            n = pstate["next_use"]
            assert plist[n] == (kind, l, i), (plist[n], kind, l, i)
            issue_loads(n + PREF)
            pstate["next_use"] += 1
            kt, W, _ = panel_parts(kind, l, i)
            sv = ring[n % NSLOT][:, 0:kt * W].rearrange("p (k c) -> p k c", k=kt)
            return sv, bring[n % NSLOT]

        def get(kind, l, i):
            return pull(kind, l, i)

        def mod_setup():
            for i in range(9):
                sv, bs = pull("modr", 0, i)
                for j in range(4):
                    jt = i * 4 + j
                    for kt in range(8):
                        mm(B[7][:, jt * 3:jt * 3 + 3], sv[:, kt, j * 128:(j + 1) * 128], scT[:, kt * 3:kt * 3 + 3],
                           kt == 0, kt == 7, [bs, bsc], [bB[7]])
            tt(modpart[:, :].rearrange("p (c k) -> p c k", k=3), B[7][:, 0:108].rearrange("p (c k) -> p c k", k=3),
               bmr[:, :].unsqueeze(2).to_broadcast([128, 36, 3]), ALU.add, [bB[7], bbmr], [bmodpart])
            ld("sp", agm_in, modpart[:, :], [bagm_in], bagm_in, reads=[bmodpart])
            P.dma("pool", lambda e: e.collective_compute("AllGather", ALU.bypass, replica_groups=[list(range(NCORES))],
                                                         ins=[agm_in.opt()], outs=[agm_out.opt()]),
                  reads=[bagm_in], writes=[bagm_out], owner=bagm_out, inc=1)
            ld("sp", modall[:, :].rearrange("p (r c) -> p r c", r=NCORES), agm_out.rearrange("(r p) c -> p r c", p=128),
               [bmodall], bmodall, reads=[bagm_out])
            for l in range(NL):
                cb = l * CL
                for j in range(3):
                    sc0 = l * 216 + ((3 * j + 1) * 8) * 3
                    stt(Aj[:, l * 72 + j * 24: l * 72 + (j + 1) * 24].rearrange("p (f c) -> p f c", c=3),
                        modall[:, sc0:sc0 + 24].rearrange("p (f c) -> p f c", c=3), 1.0,
                        consts[:, cb + j * 8: cb + (j + 1) * 8].unsqueeze(2).to_broadcast([128, 8, 3]),
                        ALU.add, ALU.mult, [bmodall, bconsts], [bder1])
                    g0 = l * 216 + ((3 * j + 2) * 8) * 3
                    ts(Gj[:, l * 72 + j * 24: l * 72 + (j + 1) * 24], modall[:, g0:g0 + 24], 1.0 if j == 1 else 0.5, None,
                       ALU.mult, None, [bmodall], [bder1])

        def A_ap(l, j, ft, c):
            k = l * 72 + j * 24 + ft * 3 + c
            return Aj[:, k:k + 1]

        def G_ap(l, j, ft, c):
            k = l * 72 + j * 24 + ft * 3 + c
            return Gj[:, k:k + 1]

        def B_ap(l, j, ft, c):
            k = l * 216 + ((3 * j) * 8 + ft) * 3 + c
            return modall[:, k:k + 1]

        def norm_stats():
            for ft in range(8):
                if ft < 4:
                    act(sq[:, ft, :], xT[:, ft, :], AF.Square, [bxT[ft]], [bbT])
                else:
                    tt(sq[:, ft, :], xT[:, ft, :], xT[:, ft, :], ALU.mult, [bxT[ft]], [bbT])
            for ci, (c0, c1) in enumerate(CH):
                for ft in range(8):
                    mm(B[ci][:, 0:c1 - c0], ones[:, :], sq[:, ft, c0:c1], ft == 0, ft == 7, [bones, bbT], [bB[ci]])
                act(tmp[ci][:, 0:c1 - c0], B[ci][:, 0:c1 - c0], AF.Ln, [bB[ci], blsm], [btmp[ci]], scale=1.0 / D, bias=eps_ap)
                act(rstd[:, c0:c1], tmp[ci][:, 0:c1 - c0], AF.Exp, [btmp[ci]], [brstd], scale=-0.5)

        def norm(l, j):
            norm_stats()
            k = 0
            for ft in range(8):
                p = ft % 2
                tt(nt[p][:, :], xT[:, ft, :], rstd[:, :], ALU.mult, [bxT[ft], brstd], [bnt[p]])
                for (g0, g1, c) in GROUPS:
                    if k % 2 == 0:
                        act(hT[:, ft, g0:g1], nt[p][:, g0:g1], AF.Identity, [bnt[p], bder[l % 2], bmod[l % 2]], [bhT],
                            scale=A_ap(l, j, ft, c), bias=B_ap(l, j, ft, c))
                    else:
                        ts(hT[:, ft, g0:g1], nt[p][:, g0:g1], A_ap(l, j, ft, c), B_ap(l, j, ft, c), ALU.mult, ALU.add,
                           [bnt[p], bder[l % 2], bmod[l % 2]], [bhT])
                    k += 1

        def residual(l, j, ft, bank0, bb0, bank1, bb1):
            for (g0, g1, c) in GROUPS:
                if g0 < 512:
                    src, bsrc = bank0[:, g0:g1], bb0
                else:
                    src, bsrc = bank1[:, g0 - 512:g1 - 512], bb1
                stt(xT[:, ft, g0:g1], src, G_ap(l, j, ft, c), xT[:, ft, g0:g1], ALU.mult, ALU.add,
                    [bsrc, bder[l % 2]], [bxT[ft]])

        def ffn(l, which):
            ka, kb = ("f1a", "f1b") if which == 0 else ("f2a", "f2b")
            j = 0 if which == 0 else 2
            for pi in range(11):
                sv, bs = get(ka, l, pi)
                for jj in range(2):
                    ht = 2 * pi + jj
                    s = ht % 2
                    Bg, Bu, Bc = B[3 * s], B[3 * s + 1], B[3 * s + 2]
                    bg, bu, bc = bB[3 * s], bB[3 * s + 1], bB[3 * s + 2]
                    for kt in range(8):
                        mm(Bg[:, :], sv[:, kt, jj * 128:(jj + 1) * 128], hT[:, kt, 0:512], kt == 0, kt == 7, [bs, bhT], [bg])
                    for kt in range(8):
                        mm(Bu[:, :], sv[:, kt, 256 + jj * 128:256 + (jj + 1) * 128], hT[:, kt, 0:512], kt == 0, kt == 7, [bs, bhT], [bu])
                    for kt in range(8):
                        mm(Bc[:, 0:256], sv[:, kt, jj * 128:(jj + 1) * 128], hT[:, kt, 512:768], kt == 0, kt == 7, [bs, bhT], [bc])
                    for kt in range(8):
                        mm(Bc[:, 256:512], sv[:, kt, 256 + jj * 128:256 + (jj + 1) * 128], hT[:, kt, 512:768], kt == 0, kt == 7, [bs, bhT], [bc])
                    t0, t1 = 2 * s, 2 * s + 1
                    act(tmp[t0][:, :], Bg[:, :], AF.Silu, [bg], [btmp[t0]])
                    tt(gT[:, ht, 0:512], tmp[t0][:, :], Bu[:, :], ALU.mult, [btmp[t0], bu], [bG])
                    act(tmp[t1][:, 0:256], Bc[:, 0:256], AF.Silu, [bc], [btmp[t1]])
                    tt(gT[:, ht, 512:768], tmp[t1][:, 0:256], Bc[:, 256:512], ALU.mult, [btmp[t1], bc], [bG])
            for pj in range(8):
                sv, bs = get(kb, l, pj)
                s = pj % 3
                for kt in range(22):
                    mm(B[2 * s][:, :], sv[:, kt, :], gT[:, kt, 0:512], kt == 0, kt == 21, [bs, bG], [bB[2 * s]])
                for kt in range(22):
                    mm(B[2 * s + 1][:, 0:256], sv[:, kt, :], gT[:, kt, 512:768], kt == 0, kt == 21, [bs, bG], [bB[2 * s + 1]])
                residual(l, j, pj, B[2 * s], bB[2 * s], B[2 * s + 1], bB[2 * s + 1])

        cnt_post = [0]
        pend = []

        def attn_post1(l, h, tok0, n, o1, s1, o2, s2, bo1, bo2):
            par = cnt_post[0] % 2
            cnt_post[0] += 1
            r1 = tmp[0][:, 0:n]; r2 = tmp[0][:, 256:256 + n]
            t1 = tmp[1][:, 0:n]; t2 = tmp[1][:, 256:256 + n]
            oo = oo_b[par][:, 0:n]
            P.op("dve", lambda e: e.reciprocal(out=r1, in_=s1), reads=[bo1], writes=[btmp[0]])
            P.op("dve", lambda e: e.reciprocal(out=r2, in_=s2), reads=[bo2], writes=[btmp[0]])
            tt(t1, o1, r1, ALU.mult, [bo1, btmp[0]], [btmp[1]])
            tt(t2, o2, r2, ALU.mult, [bo2, btmp[0]], [btmp[1]])
            stt(oo, t2, lsm[:, 16 + l:17 + l], t1, ALU.mult, ALU.add, [btmp[1], blsm], [boo[par]])
            act(osq_b[par][:, 0:n], oo, AF.Square, [boo[par]], [bosq_b[par]])
            pend.append((l, h, tok0, n, par))

        def attn_post2():
            l, h, tok0, n, par = pend.pop(0)
            oo = oo_b[par][:, 0:n]
            lt = tmp[3][:, 0:n]; rs = tmp[3][:, 256:256 + n]
            mm(B[7][:, par * 256:par * 256 + n], ones[:, :], osq_b[par][:, 0:n], True, True, [bones, bosq_b[par]], [bB[7]])
            act(lt, B[7][:, par * 256:par * 256 + n], AF.Ln, [bB[7], blsm], [btmp[3]], scale=1.0 / 128, bias=eps_ap)
            act(rs, lt, AF.Exp, [btmp[3]], [btmp[3]], scale=-0.5)
            stt(bT[:, h, tok0:tok0 + n], oo, lsm[:, 20 + l:21 + l], rs, ALU.mult, ALU.mult, [boo[par], btmp[3], blsm], [bbT])

        def attn_post(*a):
            attn_post1(*a)
            while len(pend) > 1:
                attn_post2()

        def attn_flush():
            while pend:
                attn_post2()

        def mixer(l):
            ld("sp", sgu_bc[:, :], sgu_d[l:l + 1, :].partition_broadcast(128), [bsgu], bsgu)
            ld("sp", bsp_bc[:, :], bsp_d[l:l + 1, :].partition_broadcast(128), [bbsp], bbsp)
            for pi in range(2):
                sv, bs = get("mu", l, pi)
                for jf in range(4):
                    ft = pi * 4 + jf
                    s = ft % 3
                    for kt in range(8):
                        mm(B[2 * s][:, :], sv[:, kt, jf * 128:(jf + 1) * 128], hT[:, kt, 0:512], kt == 0, kt == 7, [bs, bhT], [bB[2 * s]])
                    for kt in range(8):
                        mm(B[2 * s + 1][:, 0:256], sv[:, kt, jf * 128:(jf + 1) * 128], hT[:, kt, 512:768], kt == 0, kt == 7, [bs, bhT], [bB[2 * s + 1]])
                    act(uaT[:, ft, 0:512], B[2 * s][:, :], AF.Gelu_apprx_tanh, [bB[2 * s]], [bua])
                    act(uaT[:, ft, 512:768], B[2 * s + 1][:, 0:256], AF.Gelu_apprx_tanh, [bB[2 * s + 1]], [bua])
            sv0, bs0 = get("mv", l, 0)
            sv1, bs1 = pull("mv", l, 1)
            rot = 0
            for tk in range(6):
                p = tk % 2
                for cb, (sv, bs) in enumerate(((sv0, bs0), (sv1, bs1))):
                    bk = rot % 6
                    rot += 1
                    for kt in range(8):
                        mm(B[bk][:, :], hT[:, kt, tk * 128:(tk + 1) * 128], sv[:, kt, :], kt == 0, kt == 7, [bs, bhT], [bB[bk]])
                    act(vstg[p][:, cb * 512:(cb + 1) * 512], B[bk][:, :], AF.Gelu_apprx_tanh, [bB[bk]], [bvstg[p]])
                    stt(tmp[cb][:, :], vstg[p][:, cb * 512:(cb + 1) * 512], 1.0, vstg[p][:, cb * 512:(cb + 1) * 512],
                        ALU.mult, ALU.mult, [bvstg[p]], [btmp[cb], bssq], accum_out=ssq[:, 2 * tk + cb:2 * tk + cb + 1])
                tt(ssq[:, 12 + p:13 + p], ssq[:, 2 * tk:2 * tk + 1], ssq[:, 2 * tk + 1:2 * tk + 2], ALU.add, [bssq], [bssq])
                act(ssq[:, 14 + p:15 + p], ssq[:, 12 + p:13 + p], AF.Ln, [bssq, blsm], [bssq], scale=1.0 / D, bias=eps_ap)
                act(ssq[:, 12 + p:13 + p], ssq[:, 14 + p:15 + p], AF.Exp, [bssq], [bssq], scale=-0.5)
                stt(vn[:, tk, :], vstg[p][:, :], ssq[:, 12 + p:13 + p], sgu_bc[:, :], ALU.mult, ALU.mult,
                    [bvstg[p], bssq, bsgu], [bvn])
                for half in range(2):
                    bk = rot % 6
                    rot += 1
                    for jc in range(4):
                        ct = half * 4 + jc
                        g = ct // 2
                        mm(B[bk][:, jc * 128:(jc + 1) * 128], vn[:, tk, ct * 128:(ct + 1) * 128],
                           wsT[:, (l * 4 + g) * 128:(l * 4 + g + 1) * 128], True, True, [bvn, bws], [bB[bk]])
                    tb = 2 + half
                    tt(tmp[tb][:, :].rearrange("p (g j t) -> p g j t", g=2, j=2), B[bk][:, :].rearrange("p (g j t) -> p g j t", g=2, j=2),
                       bsp_bc[:, half * 256:(half + 1) * 256].rearrange("p (g t) -> p g t", g=2).unsqueeze(2).to_broadcast([128, 2, 2, 128]),
                       ALU.add, [bB[bk], bbsp], [btmp[tb]])
                    tt(uaT[:, half * 4:(half + 1) * 4, tk * 128:(tk + 1) * 128], tmp[tb][:, :].rearrange("p (c t) -> p c t", c=4),
                       uaT[:, half * 4:(half + 1) * 4, tk * 128:(tk + 1) * 128], ALU.mult, [btmp[tb], bua], [bua])
            cos2 = rope[:, 0:128].unsqueeze(1).to_broadcast([128, 2, 128])
            sin2 = rope[:, 128:256].unsqueeze(1).to_broadcast([128, 2, 128])

            def rope_apply(dst, bank1, bb1, bdst):
                qf = tmp[0][:, 0:256]
                acopy(qf, bank1[:, 0:256], [bb1], [btmp[0]])
                mm(bank1[:, 256:512], perm[:, :], qf, True, True, [bperm, btmp[0]], [bb1])
                tt(tmp[1][:, 0:256].rearrange("p (s t) -> p s t", s=2), qf.rearrange("p (s t) -> p s t", s=2), cos2, ALU.mult,
                   [btmp[0], brope], [btmp[1]])
                tt(tmp[1][:, 256:512].rearrange("p (s t) -> p s t", s=2), bank1[:, 256:512].rearrange("p (s t) -> p s t", s=2), sin2,
                   ALU.mult, [bb1, brope], [btmp[1]])
                tt(dst, tmp[1][:, 0:256], tmp[1][:, 256:512], ALU.add, [btmp[1]], [bdst])

            for kind in ("mq", "mk"):
                for pi in range(2):
                    sv, bs = get(kind, l, pi)
                    for jf in range(4):
                        ft = pi * 4 + jf
                        s = ft % 3
                        b0, b1 = B[2 * s], B[2 * s + 1]
                        for kt in range(8):
                            mm(b0[:, :], sv[:, kt, jf * 128:(jf + 1) * 128], hT[:, kt, 0:512], kt == 0, kt == 7, [bs, bhT], [bB[2 * s]])
                        for kt in range(8):
                            mm(b1[:, 0:256], sv[:, kt, jf * 128:(jf + 1) * 128], hT[:, kt, 512:768], kt == 0, kt == 7, [bs, bhT], [bB[2 * s + 1]])
                        if kind == "mq":
                            acopy(qT[:, ft, 0:512], b0[:, :], [bB[2 * s]], [bq])
                            rope_apply(qT[:, ft, 512:768], b1, bB[2 * s + 1], bq)
                        else:
                            p = ft % 2
                            acopy(kstg[p][:, :], b0[:, :], [bB[2 * s]], [bkstg[p]])
                            ld("sp", nkT_d[l, ft * 128:(ft + 1) * 128, :], kstg[p][:, :], [bout_k], bkstg[p], reads=[bkstg[p]])
                            vcopy(kp[:, ft, :], kstg[p][:, :], [bkstg[p]], [bkp])
                            rope_apply(ksmp[:, ft, :], b1, bB[2 * s + 1], bksmp)
            ld("sp", agk_in.rearrange("(f p) t -> p f t", p=128), ksmp[:, :, :], [bagk_in], bagk_in, reads=[bksmp])
            P.dma("pool", lambda e: e.collective_compute("AllGather", ALU.bypass, replica_groups=[list(range(NCORES))],
                                                         ins=[agk_in.opt()], outs=[agk_out.opt()]),
                  reads=[bagk_in], writes=[bagk_out], owner=bagk_out, inc=1)
            sv0, bs0 = get("mva", l, 0)
            sv1, bs1 = pull("mva", l, 1)
            rot = 0
            for tk in range(6):
                p = tk % 2
                for cb, (sv, bs) in enumerate(((sv0, bs0), (sv1, bs1))):
                    bk = rot % 6
                    rot += 1
                    for kt in range(8):
                        mm(B[bk][:, :], hT[:, kt, tk * 128:(tk + 1) * 128], sv[:, kt, :], kt == 0, kt == 7, [bs, bhT], [bB[bk]])
                    acopy(vstg[p][:, cb * 512:(cb + 1) * 512], B[bk][:, :], [bB[bk]], [bvstg[p]])
                if tk < 4:
                    ld("sp", nv_d[l, tk * 128:(tk + 1) * 128, :], vstg[p][:, :], [bout_v], bvstg[p], reads=[bvstg[p]])
                    vcopy(Vb[:, tk, :], vstg[p][:, :], [bvstg[p]], [bVb])
                else:
                    vcopy(vsmp[:, tk - 4, :], vstg[p][:, :], [bvstg[p]], [bvsmp])
            ld("sp", agv_in.rearrange("(j p) c -> p j c", p=128), vsmp[:, :, :], [bagv_in], bagv_in, reads=[bvsmp])
            P.dma("pool", lambda e: e.collective_compute("AllGather", ALU.bypass, replica_groups=[list(range(NCORES))],
                                                         ins=[agv_in.opt()], outs=[agv_out.opt()]),
                  reads=[bagv_in], writes=[bagv_out], owner=bagv_out, inc=1)
            srot = 0
            orot = 0
            hcnt = 0
            for a in range(2):
                for h in range(8):
                    ETc, bETc = (ET, bET) if hcnt % 2 == 0 else (ET2, bET2)
                    hcnt += 1
                    for i in range(2):
                        sbk = srot % 3
                        srot += 1
                        for k2 in range(2):
                            mm(B[sbk][:, k2 * 256:(k2 + 1) * 256], kp[i * 64:(i + 1) * 64, h, a * 256 + k2 * 128: a * 256 + (k2 + 1) * 128],
                               qT[i * 64:(i + 1) * 64, h, a * 256:(a + 1) * 256], True, True, [bkp, bq], [bB[sbk]])
                        act(ETc[i][:, 0:512], B[sbk][:, :], AF.Exp, [bB[sbk]], [bETc[i]], scale=0.125)
                    os_ = orot % 2
                    orot += 1
                    Bo = [B[3 + 2 * os_], B[4 + 2 * os_]]
                    bBo = [bB[3 + 2 * os_], bB[4 + 2 * os_]]
                    for i in range(2):
                        for k2 in range(2):
                            mm(Bo[i][:, 0:256], Vb[:, a * 2 + k2, h * 128:(h + 1) * 128], ETc[i][:, k2 * 256:(k2 + 1) * 256],
                               k2 == 0, k2 == 1, [bVb, bETc[i]], [bBo[i]])
                        for k2 in range(2):
                            mm(Bo[i][:, 256:512], ones[:, :], ETc[i][:, k2 * 256:(k2 + 1) * 256], k2 == 0, k2 == 1, [bones, bETc[i]], [bBo[i]])
                    attn_post(l, h, a * 256, 256, Bo[0][:, 0:256], Bo[0][:, 256:512], Bo[1][:, 0:256], Bo[1][:, 256:512], bBo[0], bBo[1])
            for s in range(2):
                for hg in range(2):
                    ld("pool", K_s[:, :, 0:256], ckT_d[l, s, hg * 512:(hg + 1) * 512, :].rearrange("(h p) t -> p h t", p=128), [bKs], bKs)
                    ld("pool", V_s[:, 0:2, :], cv_d[l, s, :, hg * 512:(hg + 1) * 512].rearrange("(j p) c -> p j c", p=128), [bVs], bVs)
                    for r in range(NCORES):
                        ld("sp", K_s[:, :, 256 + r * 128: 256 + (r + 1) * 128],
                           agk_out[r * 1024 + hg * 512: r * 1024 + (hg + 1) * 512, s * 128:(s + 1) * 128].rearrange("(h p) t -> p h t", p=128),
                           [bKs], bKs, reads=[bagk_out])
                        ld("sp", V_s[:, 2 + r, :], agv_out[r * 256 + s * 128: r * 256 + (s + 1) * 128, hg * 512:(hg + 1) * 512],
                           [bVs], bVs, reads=[bagv_out])
                    for hl in range(4):
                        h = hg * 4 + hl
                        q0 = 512 + s * 128
                        ETc, bETc = (ET, bET) if hcnt % 2 == 0 else (ET2, bET2)
                        hcnt += 1
                        for i in range(2):
                            for (t0, t1) in ((0, 4), (4, 8), (8, 10)):
                                sbk = srot % 3
                                srot += 1
                                for t in range(t0, t1):
                                    mm(B[sbk][:, (t - t0) * 128:(t - t0 + 1) * 128], K_s[i * 64:(i + 1) * 64, hl, t * 128:(t + 1) * 128],
                                       qT[i * 64:(i + 1) * 64, h, q0:q0 + 128], True, True, [bKs, bq], [bB[sbk]])
                                nn = (t1 - t0) * 128
                                act(ETc[i][:, t0 * 128:t1 * 128], B[sbk][:, 0:nn], AF.Exp, [bB[sbk]], [bETc[i]], scale=0.125)
                        ob = 3 + (orot % 4)
                        orot += 1
                        for i in range(2):
                            for t in range(10):
                                mm(B[ob][:, i * 256:i * 256 + 128], V_s[:, t, hl * 128:(hl + 1) * 128], ETc[i][:, t * 128:(t + 1) * 128],
                                   t == 0, t == 9, [bVs, bETc[i]], [bB[ob]])
                            for t in range(10):
                                mm(B[ob][:, i * 256 + 128:i * 256 + 256], ones[:, :], ETc[i][:, t * 128:(t + 1) * 128],
                                   t == 0, t == 9, [bones, bETc[i]], [bB[ob]])
                        attn_post(l, h, q0, 128, B[ob][:, 0:128], B[ob][:, 128:256], B[ob][:, 256:384], B[ob][:, 384:512], bB[ob], bB[ob])
            attn_flush()
            rot = 0
            for f in range(8):
                sv, bs = get("mb", l, f)
                for cq in range(3):
                    c0, c1 = cq * 256, (cq + 1) * 256
                    st = rot % 3
                    rot += 1
                    Bg_, Bab = B[2 * st], B[2 * st + 1]
                    bg_, bab = bB[2 * st], bB[2 * st + 1]
                    for kt in range(8):
                        mm(Bg_[:, 0:256], sv[:, kt, 0:128], hT[:, kt, c0:c1], kt == 0, kt == 7, [bs, bhT], [bg_])
                    for kt in range(8):
                        mm(Bg_[:, 256:512], sv[:, kt, 128:256], hT[:, kt, c0:c1], kt == 0, kt == 7, [bs, bhT], [bg_])
                    for kt in range(8):
                        mm(Bab[:, 0:256], sv[:, kt, 256:384], uaT[:, kt, c0:c1], kt == 0, kt == 7, [bs, bua], [bab])
                    for kt in range(8):
                        mm(Bab[:, 256:512], sv[:, kt, 384:512], bT[:, kt, c0:c1], kt == 0, kt == 7, [bs, bbT], [bab])
                    ta, tb = (rot % 2) * 2, (rot % 2) * 2 + 1
                    act(tmp[ta][:, :], Bg_[:, :], AF.Sigmoid, [bg_], [btmp[ta]])
                    tt(tmp[tb][:, :], tmp[ta][:, :], Bab[:, :], ALU.mult, [btmp[ta], bab], [btmp[tb]])
                    tt(mT[:, f, c0:c1], tmp[tb][:, 0:256], tmp[tb][:, 256:512], ALU.add, [btmp[tb]], [bq])
            for pi in range(2):
                sv, bs = get("mo", l, pi)
                for jf in range(4):
                    ft = pi * 4 + jf
                    s = ft % 3
                    for kt in range(8):
                        mm(B[2 * s][:, :], sv[:, kt, jf * 128:(jf + 1) * 128], mT[:, kt, 0:512], kt == 0, kt == 7, [bs, bq], [bB[2 * s]])
                    for kt in range(8):
                        mm(B[2 * s + 1][:, 0:256], sv[:, kt, jf * 128:(jf + 1) * 128], mT[:, kt, 512:768], kt == 0, kt == 7, [bs, bq], [bB[2 * s + 1]])
                    residual(l, 1, ft, B[2 * s], bB[2 * s], B[2 * s + 1], bB[2 * s + 1])

        ld("sp", condT[:, :], condT_d, [bcond], bcond)
        ld("sp", consts[:, :], consts_d, [bconsts], bconsts)
        ld("sp", lam_bc[:, :], lam_d.partition_broadcast(128), [blam], blam)
        ld("sp", rope[:, :], rope_d, [brope], brope)
        ld("sp", perm[:, :], perm_d, [bperm], bperm)
        ld("sp", xT[:, :, :], xT_d.rearrange("(f p) t -> p f t", p=128), bxT, bsetup)
        ld("pool", wsT[:, :], wsT_d, [bws], bws)
        P.op("dve", lambda e: e.memset(ones[:, :], 1.0), writes=[bones])
        P.op("dve", lambda e: e.memset(lsm[:, 24:25], EPS), writes=[blsm])
        act(scT[:, :], condT[:, :], AF.Silu, [bcond], [bsc])
        lv = lam_bc[:, :].rearrange("p (l a b d) -> p l a b d", l=4, a=2, b=2)
        tt(tmp[0][:, :].rearrange("p (l a d) -> p l a d", l=4, a=2), lv[:, :, :, 0, :], lv[:, :, :, 1, :], ALU.mult, [blam], [btmp[0]])
        P.op("dve", lambda e: e.tensor_reduce(out=lsm[:, 0:8], in_=tmp[0][:, :].rearrange("p (k d) -> p k d", d=64), axis=AX.X, op=ALU.add),
             reads=[btmp[0]], writes=[blsm])
        act(lsm[:, 8:16], lsm[:, 0:8], AF.Exp, [blsm], [blsm])
        le = lsm[:, 8:16].rearrange("p (l a) -> p l a", a=2)
        tt(lsm[:, 16:20], le[:, :, 1], le[:, :, 0], ALU.subtract, [blsm], [blsm])
        for l in range(4):
            ts(lsm[:, 16 + l:17 + l], lsm[:, 16 + l:17 + l], -lam_init(l), None, ALU.add, None, [blsm], [blsm])
            ts(lsm[:, 20 + l:21 + l], consts[:, l * CL + 24:l * CL + 25], 1.0 - lam_init(l), None, ALU.mult, None, [bconsts, blsm], [blsm])
        ld("sp", bmr[:, :], bmod_r, [bbmr], bbmr)
        mod_setup()
        for l in range(NL):
            norm(l, 0)
            ffn(l, 0)
            norm(l, 1)
            mixer(l)
            norm(l, 2)
            ffn(l, 1)
        norm_stats()
        gf = 4 * CL
        for ft in range(8):
            p = ft % 2
            stt(nt[p][:, :], xT[:, ft, :], consts[:, gf + ft:gf + ft + 1], rstd[:, :], ALU.mult, ALU.mult,
                [bxT[ft], bconsts, brstd], [bnt[p]])
            ld("sp", yT_d[ft * 128:(ft + 1) * 128, :], nt[p][:, :], [bout_y], bnt[p], reads=[bnt[p]])
        P.wait_all("sp", [bout_y, bout_k, bout_v])
        assert pstate["next_use"] == len(plist), (pstate, len(plist))
        P.emit()
    return nc


def _rope_tables(r):
    pos = r * 128 + np.arange(128)
    row = (pos // 64).astype(np.float32)
    col = (pos % 64).astype(np.float32)
    freqs = (np.float32(10000.0) ** (-np.arange(0, 32, 2, dtype=np.float32) / np.float32(32))).astype(np.float32)
    cos = np.zeros((128, 128), np.float32)
    sin = np.zeros((128, 128), np.float32)
    for p in range(128):
        d = p % 64
        jj = d % 32
        base = row if d < 32 else col
        ang = (base * freqs[jj % 16]).astype(np.float32)
        cos[p] = np.cos(ang)
        sin[p] = np.sin(ang) * (-1.0 if jj < 16 else 1.0)
    return np.concatenate([cos, sin], axis=1).astype(np.float32)


def _perm_matrix():
    m = np.zeros((128, 128), np.float32)
    for p in range(128):
        jj = (p % 64) % 32
        partner = p + 16 if jj < 16 else p - 16
        m[partner, p] = 1.0
    return m


_NC_CACHE = {}


def kernel(x_prompt, x_sample, cache_k, cache_v, c, c_ctx, w_mod, b_mod, g_norm, ffn1_w13, ffn1_w2, w_in, sgu_gain,
           w_spatial, b_spatial, lam, subln_gain, w_branch_a, w_branch_b, w_out, ffn2_w13, ffn2_w2, g_final, _NL=4):
    f = lambda a: np.ascontiguousarray(np.asarray(a, dtype=np.float32))
    x_prompt, x_sample, cache_k, cache_v = f(x_prompt), f(x_sample), f(cache_k), f(cache_v)
    NL = _NL
    if NL not in _NC_CACHE:
        _NC_CACHE[NL] = build(NL)
    nc = _NC_CACHE[NL]
    conds = np.stack([f(c_ctx), f(c)[0], f(c)[1]], axis=1)
    condT = np.ascontiguousarray(conds.reshape(8, 128, 3).transpose(1, 0, 2).reshape(128, 24))
    consts = np.zeros((128, NCONST), np.float32)
    for l in range(4):
        consts[:, l * CL:l * CL + 24] = f(g_norm)[l].reshape(24, 128).T
        consts[:, l * CL + 24] = f(subln_gain)[l]
    consts[:, 4 * CL:4 * CL + 8] = f(g_final).reshape(8, 128).T
    wsT = np.ascontiguousarray(f(w_spatial).transpose(3, 0, 1, 2).reshape(128, 2048))
    ckT = np.ascontiguousarray(cache_k.reshape(2, 4, 256, 1024).transpose(1, 0, 3, 2))
    cv = np.ascontiguousarray(cache_v.reshape(2, 4, 256, 1024).transpose(1, 0, 2, 3))
    shared = {
        "condT": condT, "consts": consts, "sgu": f(sgu_gain), "bsp": f(b_spatial).reshape(4, 512),
        "lam": f(lam).reshape(1, 1024), "wsT": wsT, "perm": _perm_matrix(),
        "ffn1_w13": f(ffn1_w13), "ffn1_w2": f(ffn1_w2), "w_in": f(w_in),
        "w_branch_a": f(w_branch_a), "w_branch_b": f(w_branch_b), "w_out": f(w_out),
        "ffn2_w13": f(ffn2_w13), "ffn2_w2": f(ffn2_w2), "ckT": ckT, "cv": cv,
    }
    in_maps = []
    for r in range(NCORES):
        xc = np.concatenate([x_prompt[2 * r], x_prompt[2 * r + 1], x_sample[0, r * 128:(r + 1) * 128],
                             x_sample[1, r * 128:(r + 1) * 128]], axis=0)
        m = dict(shared)
        m["xT"] = np.ascontiguousarray(xc.T)
        m["rope"] = _rope_tables(r)
        m["wmod_r"] = np.ascontiguousarray(f(w_mod)[r // 2][:, (r % 2) * 4608:(r % 2 + 1) * 4608])
        m["bmod_r"] = np.ascontiguousarray(f(b_mod)[r // 2][(r % 2) * 4608:(r % 2 + 1) * 4608].reshape(36, 128).T)
        in_maps.append(m)
    res = run_bass_kernel_spmd(nc, in_maps, core_ids=list(range(NCORES)))
    y_prompt = np.zeros((16, 256, 1024), np.float32)
    y_sample = np.zeros((2, 1024, 1024), np.float32)
    nk = np.zeros((16, NL, 256, 8, 128), np.float32)
    nv = np.zeros((16, NL, 256, 8, 128), np.float32)
    for r in range(NCORES):
        o = res.results[r]
        y = np.asarray(o["yT"]).T
        y_prompt[2 * r] = y[0:256]
        y_prompt[2 * r + 1] = y[256:512]
        y_sample[0, r * 128:(r + 1) * 128] = y[512:640]
        y_sample[1, r * 128:(r + 1) * 128] = y[640:768]
        k_ = np.asarray(o["nkT"])
        v_ = np.asarray(o["nv"])
        for l in range(NL):
            kt = k_[l].T
            for a in range(2):
                nk[2 * r + a, l] = kt[a * 256:(a + 1) * 256].reshape(256, 8, 128)
                nv[2 * r + a, l] = v_[l][a * 256:(a + 1) * 256].reshape(256, 8, 128)
    return (y_prompt, y_sample, nk, nv)
```

```python
import contextlib
import math
import numpy as np
import concourse.bass as bass
import concourse.mybir as mybir
from concourse.bass_utils import run_bass_kernel_spmd

F32 = mybir.dt.float32
BF16 = mybir.dt.bfloat16
AF = mybir.ActivationFunctionType
ALU = mybir.AluOpType
AX = mybir.AxisListType

NCORES = 8
D = 1024
DFF = 2816
T = 768
EPS = 1e-6
EPOCH = 4000
ENGS = ("pe", "act", "dve", "pool", "sp")
GROUPS = [(0, 512, 0), (512, 640, 1), (640, 768, 2)]
CH = [(0, 512), (512, 768)]
NSLOT = 4
PREF = 2
CL = 25
NCONST = 4 * CL + 8


def lam_init(l):
    return 0.8 - 0.6 * math.exp(-0.3 * l)


class Buf:
    __slots__ = ("name", "w", "r", "dkey", "dcnt", "over")

    def __init__(self, name):
        self.name = name
        self.w = {}
        self.r = {}
        self.dkey = None
        self.dcnt = 0
        self.over = []


class Prog:
    def __init__(self, nc, stack):
        self.nc = nc
        self.stack = stack
        self.streams = {e: [] for e in ENGS}
        self.count = {e: 0 for e in ENGS}
        self.seen = {e: {} for e in ENGS}
        self.sems = {}

    def sem(self, key):
        s = self.sems.get(key)
        if s is None:
            s = self.stack.enter_context(self.nc.semaphore("s_%s" % "_".join(str(k) for k in key)))
            self.sems[key] = s
        return s

    def sb(self, name, shape, dt):
        return self.stack.enter_context(self.nc.sbuf_tensor("sb_" + name, list(shape), dt))

    def ps(self, name, shape, dt=F32):
        return self.stack.enter_context(self.nc.psum_tensor("ps_" + name, list(shape), dt))

    def _deps(self, e, reads, writes):
        waits = {}

        def need(k, v):
            if self.seen[e].get(k, 0) < v:
                self.seen[e][k] = v
                waits[k] = max(waits.get(k, 0), v)

        for b in reads:
            for k, v in b.w.items():
                if e == "pe" and k[0] == "pe":
                    continue
                need(k, v)
        for b0 in writes:
            for b in [b0] + b0.over:
                for k, v in b.w.items():
                    if k[0] == e:
                        continue
                    need(k, v)
                for k, v in b.r.items():
                    if k[0] == e:
                        continue
                    need(k, v)
        for k, v in waits.items():
            self.streams[e].append(("wait", k, v))

    def _mark(self, tok, reads, writes):
        k, v = tok
        for b in writes:
            if b.w.get(k, 0) < v:
                b.w[k] = v
        for b in reads:
            if b.r.get(k, 0) < v:
                b.r[k] = v

    def op(self, e, fn, reads=(), writes=()):
        self._deps(e, reads, writes)
        self.count[e] += 1
        c = self.count[e]
        ep = (c - 1) // EPOCH
        tok = ((e, ep), c - ep * EPOCH)
        self.sem(tok[0])
        self.streams[e].append(("ins", fn, tok[0], 1))
        self._mark(tok, reads, writes)
        return tok

    def dma(self, q, fn, reads=(), writes=(), owner=None, inc=16):
        self._deps(q, reads, writes)
        b = owner
        if b.dkey is None or b.dcnt + inc > 3500:
            b.dkey = ("d", b.name, 0 if b.dkey is None else b.dkey[2] + 1)
            b.dcnt = 0
            self.sem(b.dkey)
        b.dcnt += inc
        tok = (b.dkey, b.dcnt)
        self.streams[q].append(("ins", fn, b.dkey, inc))
        self._mark(tok, reads, writes)
        return tok

    def wait_all(self, e, bufs):
        waits = {}
        for b in bufs:
            for d in (b.w, b.r):
                for k, v in d.items():
                    if self.seen[e].get(k, 0) < v:
                        self.seen[e][k] = v
                        waits[k] = max(waits.get(k, 0), v)
        for k, v in waits.items():
            self.streams[e].append(("wait", k, v))

    def emit(self):
        nc = self.nc
        P = self

        def run(eng, items):
            for it in items:
                if it[0] == "wait":
                    eng.wait_ge(P.sems[it[1]], it[2])
                else:
                    it[1](eng).then_inc(P.sems[it[2]], it[3])

        with nc.Block() as block:
            @block.sync
            def _(eng):
                run(eng, P.streams["sp"])

            @block.tensor
            def _(eng):
                run(eng, P.streams["pe"])

            @block.scalar
            def _(eng):
                run(eng, P.streams["act"])

            @block.vector
            def _(eng):
                run(eng, P.streams["dve"])

            @block.gpsimd
            def _(eng):
                run(eng, P.streams["pool"])


def build(NL=4):
    nc = bass.Bass("TRN2", target_bir_lowering=False)

    def din(name, shape, dt=F32):
        return nc.dram_tensor(name, list(shape), dt, kind="ExternalInput").ap()

    def dout(name, shape, dt=F32):
        return nc.dram_tensor(name, list(shape), dt, kind="ExternalOutput").ap()

    xT_d = din("xT", [D, T])
    condT_d = din("condT", [128, 24])
    consts_d = din("consts", [128, NCONST])
    sgu_d = din("sgu", [4, 1024])
    bsp_d = din("bsp", [4, 512])
    lam_d = din("lam", [1, 1024])
    wsT_d = din("wsT", [128, 2048])
    rope_d = din("rope", [128, 256])
    perm_d = din("perm", [128, 128])
    wmod_r = din("wmod_r", [1024, 4608])
    bmod_r = din("bmod_r", [128, 36])
    f1_w13 = din("ffn1_w13", [4, 1024, 2 * DFF])
    f1_w2 = din("ffn1_w2", [4, DFF, 1024])
    w_in = din("w_in", [4, 1024, 7168])
    w_ba = din("w_branch_a", [4, 1024, 1024])
    w_bb = din("w_branch_b", [4, 1024, 1024])
    w_o = din("w_out", [4, 1024, 1024])
    f2_w13 = din("ffn2_w13", [4, 1024, 2 * DFF])
    f2_w2 = din("ffn2_w2", [4, DFF, 1024])
    ckT_d = din("ckT", [4, 2, 1024, 256])
    cv_d = din("cv", [4, 2, 256, 1024])
    yT_d = dout("yT", [D, T])
    nkT_d = dout("nkT", [4, 1024, 512])
    nv_d = dout("nv", [4, 512, 1024])
    agk_in = nc.dram_tensor("agk_in", [1024, 256], BF16).ap()
    agk_out = nc.dram_tensor("agk_out", [NCORES * 1024, 256], BF16).ap()
    agv_in = nc.dram_tensor("agv_in", [256, 1024], BF16).ap()
    agv_out = nc.dram_tensor("agv_out", [NCORES * 256, 1024], BF16).ap()
    agm_in = nc.dram_tensor("agm_in", [128, 108], F32).ap()
    agm_out = nc.dram_tensor("agm_out", [NCORES * 128, 108], F32).ap()

    with contextlib.ExitStack() as stack:
        P = Prog(nc, stack)
        xT = P.sb("xT", [128, 8, T], F32)
        bxT = [Buf("xT%d" % i) for i in range(8)]
        hT = P.sb("hT", [128, 8, T], BF16); bhT = Buf("hT")
        R = P.sb("R", [128, 45568], BF16)
        gT = R[:, 0:16896].rearrange("p (k t) -> p k t", k=22); bG = Buf("G")
        uaT = R[:, 0:6144].rearrange("p (k t) -> p k t", k=8)
        vn = R[:, 6144:12288].rearrange("p (k c) -> p k c", k=6)
        qT = R[:, 12288:18432].rearrange("p (k t) -> p k t", k=8)
        mT = qT
        bT = R[:, 18432:24576].rearrange("p (k t) -> p k t", k=8); bbT = Buf("bT")
        sq = bT
        kp = R[:, 24576:28672].rearrange("p (k t) -> p k t", k=8); bkp = Buf("kp")
        Vb = R[:, 28672:32768].rearrange("p (k c) -> p k c", k=4); bVb = Buf("Vb")
        ET = [R[:, 32768 + i * 1280: 32768 + (i + 1) * 1280] for i in range(2)]
        bET = [Buf("ET%d" % i) for i in range(2)]
        K_s = R[:, 35328:40448].rearrange("p (k t) -> p k t", k=4); bKs = Buf("Ks")
        V_s = R[:, 40448:45568].rearrange("p (k c) -> p k c", k=10); bVs = Buf("Vs")
        bua = Buf("ua"); bq = Buf("q")
        ET2 = [R[:, 6144 + i * 1280: 6144 + (i + 1) * 1280] for i in range(2)]
        bET2 = [Buf("ET2_%d" % i) for i in range(2)]
        oo_b = [R[:, 8704 + i * 512: 8704 + (i + 1) * 512].bitcast(F32) for i in range(3)]
        boo = [Buf("oo%d" % i) for i in range(3)]
        osq_b = [R[:, 10240 + i * 256: 10240 + (i + 1) * 256] for i in range(3)]
        bosq_b = [Buf("osq%d" % i) for i in range(3)]
        vn_alias = bET2 + boo + bosq_b
        bvn_t = [Buf("vn%d" % i) for i in range(6)]
        bG.over = [bua, bq] + bvn_t + vn_alias
        bua.over = [bG]; bq.over = [bG]
        for b_ in bvn_t:
            b_.over = [bG] + vn_alias
        for b_ in vn_alias:
            b_.over = bvn_t + [bG]
        ring = [P.sb("ring%d" % i, [128, 4096], BF16) for i in range(NSLOT)]
        bring = [Buf("ring%d" % i) for i in range(NSLOT)]
        vstg = [P.sb("vstg%d" % i, [128, 1024], F32) for i in range(2)]
        bvstg = [Buf("vstg%d" % i) for i in range(2)]
        kstg = [P.sb("kstg%d" % i, [128, 512], F32) for i in range(2)]
        bkstg = [Buf("kstg%d" % i) for i in range(2)]
        ksmp = P.sb("ksmp", [128, 8, 256], BF16); bksmp = Buf("ksmp")
        vsmp = P.sb("vsmp", [128, 2, 1024], BF16); bvsmp = Buf("vsmp")
        rstd = P.sb("rstd", [128, T], F32); brstd = Buf("rstd")
        nt = [vstg[i][:, 0:T] for i in range(2)]
        bnt = bvstg
        tmp = [P.sb("tmp%d" % i, [128, 512], F32) for i in range(4)]
        btmp = [Buf("tmp%d" % i) for i in range(4)]
        consts = P.sb("consts", [128, NCONST], F32); bconsts = Buf("consts")
        condT = P.sb("condT", [128, 24], F32); bcond = Buf("condT")
        scT = P.sb("scT", [128, 24], BF16); bsc = Buf("scT")
        modall = P.sb("modall", [128, 8 * 108], F32); bmodall = Buf("modall")
        modpart = P.sb("modpart", [128, 108], F32); bmodpart = Buf("modpart")
        bmr = P.sb("bmr", [128, 36], F32); bbmr = Buf("bmr")
        Aj = P.sb("Aj", [128, 4 * 72], F32)
        Gj = P.sb("Gj", [128, 4 * 72], F32)
        bder1 = Buf("der")
        bmod = [bmodall, bmodall]
        bder = [bder1, bder1]
        bagm_in = Buf("agm_in"); bagm_out = Buf("agm_out")
        sgu_bc = P.sb("sgu_bc", [128, 1024], F32); bsgu = Buf("sgu")
        bsp_bc = P.sb("bsp_bc", [128, 512], F32); bbsp = Buf("bsp")
        lam_bc = vstg[0]; blam = bvstg[0]
        lsm = P.sb("lsm", [128, 32], F32); blsm = Buf("lsm")
        wsT = P.sb("wsT", [128, 2048], BF16); bws = Buf("wsT")
        rope = P.sb("rope", [128, 256], F32); brope = Buf("rope")
        perm = P.sb("perm", [128, 128], F32); bperm = Buf("perm")
        ones = P.sb("ones", [128, 128], BF16); bones = Buf("ones")
        ssq = P.sb("ssq", [128, 16], F32); bssq = Buf("ssq")
        B = [P.ps("bank%d" % i, [128, 512], F32) for i in range(8)]
        bB = [Buf("bank%d" % i) for i in range(8)]
        bsetup = Buf("setup")
        bout_y = Buf("out_y"); bout_k = Buf("out_k"); bout_v = Buf("out_v")
        bagk_in = Buf("agk_in"); bagk_out = Buf("agk_out"); bagv_in = Buf("agv_in"); bagv_out = Buf("agv_out")
        eps_ap = lsm[:, 24:25]

        def mm(out, lhsT, rhs, start, stop, reads, writes):
            P.op("pe", lambda e: e.matmul(out, lhsT=lhsT, rhs=rhs, start=start, stop=stop), reads=reads, writes=writes)

        def act(out, in_, func, reads, writes, scale=1.0, bias=None):
            if bias is None:
                P.op("act", lambda e: e.activation(out=out, in_=in_, func=func, scale=scale), reads=reads, writes=writes)
            else:
                P.op("act", lambda e: e.activation(out=out, in_=in_, func=func, scale=scale, bias=bias), reads=reads, writes=writes)

        def tt(out, in0, in1, op, reads, writes):
            P.op("dve", lambda e: e.tensor_tensor(out=out, in0=in0, in1=in1, op=op), reads=reads, writes=writes)

        def stt(out, in0, scalar, in1, op0, op1, reads, writes, accum_out=None):
            if accum_out is None:
                P.op("dve", lambda e: e.scalar_tensor_tensor(out=out, in0=in0, scalar=scalar, in1=in1, op0=op0, op1=op1),
                     reads=reads, writes=writes)
            else:
                P.op("dve", lambda e: e.scalar_tensor_tensor(out=out, in0=in0, scalar=scalar, in1=in1, op0=op0, op1=op1,
                                                              accum_out=accum_out), reads=reads, writes=writes)

        def ts(out, in0, s1, s2, op0, op1, reads, writes):
            if s2 is None:
                P.op("dve", lambda e: e.tensor_scalar(out=out, in0=in0, scalar1=s1, scalar2=None, op0=op0), reads=reads, writes=writes)
            else:
                P.op("dve", lambda e: e.tensor_scalar(out=out, in0=in0, scalar1=s1, scalar2=s2, op0=op0, op1=op1),
                     reads=reads, writes=writes)

        def vcopy(out, in_, reads, writes):
            P.op("dve", lambda e: e.tensor_copy(out=out, in_=in_), reads=reads, writes=writes)

        def acopy(out, in_, reads, writes):
            P.op("act", lambda e: e.copy(out=out, in_=in_), reads=reads, writes=writes)

        def ld(q, out, in_, writes, owner, reads=()):
            P.dma(q, lambda e: e.dma_start(out=out, in_=in_), reads=reads, writes=writes, owner=owner)

        def w_rows(ap2d):
            return ap2d.rearrange("(k p) c -> p k c", p=128)

        def panel_parts(kind, l, i):
            if kind == "modr":
                return 8, 512, [(wmod_r[:, i * 512:(i + 1) * 512], 0)]
            if kind in ("f1a", "f2a"):
                w = f1_w13 if kind == "f1a" else f2_w13
                return 8, 512, [(w[l, :, i * 256:(i + 1) * 256], 0), (w[l, :, DFF + i * 256: DFF + (i + 1) * 256], 256)]
            if kind in ("f1b", "f2b"):
                w = f1_w2 if kind == "f1b" else f2_w2
                return 22, 128, [(w[l, :, i * 128:(i + 1) * 128], 0)]
            if kind in ("mu", "mv", "mq", "mk", "mva"):
                base = {"mu": 0, "mv": 1024, "mq": 2048, "mk": 3072, "mva": 4096}[kind]
                return 8, 512, [(w_in[l, :, base + i * 512: base + (i + 1) * 512], 0)]
            if kind == "mb":
                return 8, 512, [(w_in[l, :, 5120 + i * 128: 5120 + (i + 1) * 128], 0),
                                (w_in[l, :, 6144 + i * 128: 6144 + (i + 1) * 128], 128),
                                (w_ba[l, :, i * 128:(i + 1) * 128], 256),
                                (w_bb[l, :, i * 128:(i + 1) * 128], 384)]
            if kind == "mo":
                return 8, 512, [(w_o[l, :, i * 512:(i + 1) * 512], 0)]
            raise ValueError(kind)

        def layer_units(l):
            U = []
            for k1, k2 in (("f1a", "f1b"),):
                U += [[(k1, l, i)] for i in range(11)] + [[(k2, l, i)] for i in range(8)]
            U += [[("mu", l, 0)], [("mu", l, 1)], [("mv", l, 0), ("mv", l, 1)], [("mq", l, 0)], [("mq", l, 1)],
                  [("mk", l, 0)], [("mk", l, 1)], [("mva", l, 0), ("mva", l, 1)]]
            U += [[("mb", l, i)] for i in range(8)]
            U += [[("mo", l, 0)], [("mo", l, 1)]]
            U += [[("f2a", l, i)] for i in range(11)] + [[("f2b", l, i)] for i in range(8)]
            return U

        plist = [("modr", 0, i) for i in range(9)]
        for l in range(NL):
            for u in layer_units(l):
                plist += u
        pstate = {"next_use": 0, "next_load": 0}

        def issue_loads(upto):
            while pstate["next_load"] <= min(upto, len(plist) - 1):
                n = pstate["next_load"]
                kind, l, i = plist[n]
                kt, W, parts = panel_parts(kind, l, i)
                slot = ring[n % NSLOT]
                sv = slot[:, 0:kt * W].rearrange("p (k c) -> p k c", k=kt)
                for (src, off) in parts:
                    nco = src.shape[-1]
                    ld("pool", sv[:, :, off:off + nco], w_rows(src), [bring[n % NSLOT]], bring[n % NSLOT])
                pstate["next_load"] += 1

        def pull(kind, l, i):
            n = pstate["next_use"]
            assert plist[n] == (kind, l, i), (plist[n], kind, l, i)
            issue_loads(n + PREF)
            pstate["next_use"] += 1
            kt, W, _ = panel_parts(kind, l, i)
            sv = ring[n % NSLOT][:, 0:kt * W].rearrange("p (k c) -> p k c", k=kt)
            return sv, bring[n % NSLOT]

        def get(kind, l, i):
            return pull(kind, l, i)

        def mod_setup():
            for i in range(9):
                sv, bs = pull("modr", 0, i)
                for j in range(4):
                    jt = i * 4 + j
                    for kt in range(8):
                        mm(B[7][:, jt * 3:jt * 3 + 3], sv[:, kt, j * 128:(j + 1) * 128], scT[:, kt * 3:kt * 3 + 3],
                           kt == 0, kt == 7, [bs, bsc], [bB[7]])
            tt(modpart[:, :].rearrange("p (c k) -> p c k", k=3), B[7][:, 0:108].rearrange("p (c k) -> p c k", k=3),
               bmr[:, :].unsqueeze(2).to_broadcast([128, 36, 3]), ALU.add, [bB[7], bbmr], [bmodpart])
            ld("sp", agm_in, modpart[:, :], [bagm_in], bagm_in, reads=[bmodpart])
            P.dma("pool", lambda e: e.collective_compute("AllGather", ALU.bypass, replica_groups=[list(range(NCORES))],
                                                         ins=[agm_in.opt()], outs=[agm_out.opt()]),
                  reads=[bagm_in], writes=[bagm_out], owner=bagm_out, inc=1)
            ld("sp", modall[:, :].rearrange("p (r c) -> p r c", r=NCORES), agm_out.rearrange("(r p) c -> p r c", p=128),
               [bmodall], bmodall, reads=[bagm_out])
            for l in range(NL):
                cb = l * CL
                for j in range(3):
                    sc0 = l * 216 + ((3 * j + 1) * 8) * 3
                    stt(Aj[:, l * 72 + j * 24: l * 72 + (j + 1) * 24].rearrange("p (f c) -> p f c", c=3),
                        modall[:, sc0:sc0 + 24].rearrange("p (f c) -> p f c", c=3), 1.0,
                        consts[:, cb + j * 8: cb + (j + 1) * 8].unsqueeze(2).to_broadcast([128, 8, 3]),
                        ALU.add, ALU.mult, [bmodall, bconsts], [bder1])
                    g0 = l * 216 + ((3 * j + 2) * 8) * 3
                    ts(Gj[:, l * 72 + j * 24: l * 72 + (j + 1) * 24], modall[:, g0:g0 + 24], 1.0 if j == 1 else 0.5, None,
                       ALU.mult, None, [bmodall], [bder1])

        def A_ap(l, j, ft, c):
            k = l * 72 + j * 24 + ft * 3 + c
            return Aj[:, k:k + 1]

        def G_ap(l, j, ft, c):
            k = l * 72 + j * 24 + ft * 3 + c
            return Gj[:, k:k + 1]

        def B_ap(l, j, ft, c):
            k = l * 216 + ((3 * j) * 8 + ft) * 3 + c
            return modall[:, k:k + 1]

        def norm_stats():
            for ft in range(8):
                if ft < 4:
                    act(sq[:, ft, :], xT[:, ft, :], AF.Square, [bxT[ft]], [bbT])
                else:
                    tt(sq[:, ft, :], xT[:, ft, :], xT[:, ft, :], ALU.mult, [bxT[ft]], [bbT])
            for ci, (c0, c1) in enumerate(CH):
                for ft in range(8):
                    mm(B[ci][:, 0:c1 - c0], ones[:, :], sq[:, ft, c0:c1], ft == 0, ft == 7, [bones, bbT], [bB[ci]])
                act(tmp[ci][:, 0:c1 - c0], B[ci][:, 0:c1 - c0], AF.Ln, [bB[ci], blsm], [btmp[ci]], scale=1.0 / D, bias=eps_ap)
                act(rstd[:, c0:c1], tmp[ci][:, 0:c1 - c0], AF.Exp, [btmp[ci]], [brstd], scale=-0.5)

        def norm(l, j):
            norm_stats()
            k = 0
            for ft in range(8):
                p = ft % 2
                tt(nt[p][:, :], xT[:, ft, :], rstd[:, :], ALU.mult, [bxT[ft], brstd], [bnt[p]])
                for (g0, g1, c) in GROUPS:
                    if k % 2 == 0:
                        act(hT[:, ft, g0:g1], nt[p][:, g0:g1], AF.Identity, [bnt[p], bder[l % 2], bmod[l % 2]], [bhT],
                            scale=A_ap(l, j, ft, c), bias=B_ap(l, j, ft, c))
                    else:
                        ts(hT[:, ft, g0:g1], nt[p][:, g0:g1], A_ap(l, j, ft, c), B_ap(l, j, ft, c), ALU.mult, ALU.add,
                           [bnt[p], bder[l % 2], bmod[l % 2]], [bhT])
                    k += 1

        def residual(l, j, ft, bank0, bb0, bank1, bb1):
            for (g0, g1, c) in GROUPS:
                if g0 < 512:
                    src, bsrc = bank0[:, g0:g1], bb0
                else:
                    src, bsrc = bank1[:, g0 - 512:g1 - 512], bb1
                stt(xT[:, ft, g0:g1], src, G_ap(l, j, ft, c), xT[:, ft, g0:g1], ALU.mult, ALU.add,
                    [bsrc, bder[l % 2]], [bxT[ft]])

        def ffn(l, which):
            ka, kb = ("f1a", "f1b") if which == 0 else ("f2a", "f2b")
            j = 0 if which == 0 else 2
            for pi in range(11):
                sv, bs = get(ka, l, pi)
                for jj in range(2):
                    ht = 2 * pi + jj
                    s = ht % 2
                    Bg, Bu, Bc = B[3 * s], B[3 * s + 1], B[3 * s + 2]
                    bg, bu, bc = bB[3 * s], bB[3 * s + 1], bB[3 * s + 2]
                    for kt in range(8):
                        mm(Bg[:, :], sv[:, kt, jj * 128:(jj + 1) * 128], hT[:, kt, 0:512], kt == 0, kt == 7, [bs, bhT], [bg])
                    for kt in range(8):
                        mm(Bu[:, :], sv[:, kt, 256 + jj * 128:256 + (jj + 1) * 128], hT[:, kt, 0:512], kt == 0, kt == 7, [bs, bhT], [bu])
                    for kt in range(8):
                        mm(Bc[:, 0:256], sv[:, kt, jj * 128:(jj + 1) * 128], hT[:, kt, 512:768], kt == 0, kt == 7, [bs, bhT], [bc])
                    for kt in range(8):
                        mm(Bc[:, 256:512], sv[:, kt, 256 + jj * 128:256 + (jj + 1) * 128], hT[:, kt, 512:768], kt == 0, kt == 7, [bs, bhT], [bc])
                    t0, t1 = 2 * s, 2 * s + 1
                    act(tmp[t0][:, :], Bg[:, :], AF.Silu, [bg], [btmp[t0]])
                    tt(gT[:, ht, 0:512], tmp[t0][:, :], Bu[:, :], ALU.mult, [btmp[t0], bu], [bG])
                    act(tmp[t1][:, 0:256], Bc[:, 0:256], AF.Silu, [bc], [btmp[t1]])
                    tt(gT[:, ht, 512:768], tmp[t1][:, 0:256], Bc[:, 256:512], ALU.mult, [btmp[t1], bc], [bG])
            for pj in range(8):
                sv, bs = get(kb, l, pj)
                s = pj % 3
                for kt in range(22):
                    mm(B[2 * s][:, :], sv[:, kt, :], gT[:, kt, 0:512], kt == 0, kt == 21, [bs, bG], [bB[2 * s]])
                for kt in range(22):
                    mm(B[2 * s + 1][:, 0:256], sv[:, kt, :], gT[:, kt, 512:768], kt == 0, kt == 21, [bs, bG], [bB[2 * s + 1]])
                residual(l, j, pj, B[2 * s], bB[2 * s], B[2 * s + 1], bB[2 * s + 1])

        def mixer(l):
            ld("sp", sgu_bc[:, :], sgu_d[l:l + 1, :].partition_broadcast(128), [bsgu], bsgu)
            ld("sp", bsp_bc[:, :], bsp_d[l:l + 1, :].partition_broadcast(128), [bbsp], bbsp)
            for pi in range(2):
                sv, bs = get("mu", l, pi)
                for jf in range(4):
                    ft = pi * 4 + jf
                    s = ft % 3
                    for kt in range(8):
                        mm(B[2 * s][:, :], sv[:, kt, jf * 128:(jf + 1) * 128], hT[:, kt, 0:512], kt == 0, kt == 7, [bs, bhT], [bB[2 * s]])
                    for kt in range(8):
                        mm(B[2 * s + 1][:, 0:256], sv[:, kt, jf * 128:(jf + 1) * 128], hT[:, kt, 512:768], kt == 0, kt == 7, [bs, bhT], [bB[2 * s + 1]])
                    act(uaT[:, ft, 0:512], B[2 * s][:, :], AF.Gelu_apprx_tanh, [bB[2 * s]], [bua])
                    act(uaT[:, ft, 512:768], B[2 * s + 1][:, 0:256], AF.Gelu_apprx_tanh, [bB[2 * s + 1]], [bua])
            rot_sp = [0]

            def spatial(tk):
                for half in range(2):
                    bk = 6 if rot_sp[0] % 2 == 0 else 7
                    rot_sp[0] += 1
                    for jc in range(4):
                        ct = half * 4 + jc
                        g = ct // 2
                        mm(B[bk][:, jc * 128:(jc + 1) * 128], vn[:, tk, ct * 128:(ct + 1) * 128],
                           wsT[:, (l * 4 + g) * 128:(l * 4 + g + 1) * 128], True, True, [bvn_t[tk], bws], [bB[bk]])
                    tb = 2 + half
                    tt(tmp[tb][:, :].rearrange("p (g j t) -> p g j t", g=2, j=2), B[bk][:, :].rearrange("p (g j t) -> p g j t", g=2, j=2),
                       bsp_bc[:, half * 256:(half + 1) * 256].rearrange("p (g t) -> p g t", g=2).unsqueeze(2).to_broadcast([128, 2, 2, 128]),
                       ALU.add, [bB[bk], bbsp], [btmp[tb]])
                    tt(uaT[:, half * 4:(half + 1) * 4, tk * 128:(tk + 1) * 128], tmp[tb][:, :].rearrange("p (c t) -> p c t", c=4),
                       uaT[:, half * 4:(half + 1) * 4, tk * 128:(tk + 1) * 128], ALU.mult, [btmp[tb], bua], [bua])

            sv0, bs0 = get("mv", l, 0)
            sv1, bs1 = pull("mv", l, 1)
            rot = 0
            for tk in range(6):
                p = tk % 2
                for cb, (sv, bs) in enumerate(((sv0, bs0), (sv1, bs1))):
                    bk = rot % 6
                    rot += 1
                    for kt in range(8):
                        mm(B[bk][:, :], hT[:, kt, tk * 128:(tk + 1) * 128], sv[:, kt, :], kt == 0, kt == 7, [bs, bhT], [bB[bk]])
                    act(vstg[p][:, cb * 512:(cb + 1) * 512], B[bk][:, :], AF.Gelu_apprx_tanh, [bB[bk]], [bvstg[p]])
                    stt(tmp[cb][:, :], vstg[p][:, cb * 512:(cb + 1) * 512], 1.0, vstg[p][:, cb * 512:(cb + 1) * 512],
                        ALU.mult, ALU.mult, [bvstg[p]], [btmp[cb], bssq], accum_out=ssq[:, 2 * tk + cb:2 * tk + cb + 1])
                tt(ssq[:, 12 + p:13 + p], ssq[:, 2 * tk:2 * tk + 1], ssq[:, 2 * tk + 1:2 * tk + 2], ALU.add, [bssq], [bssq])
                act(ssq[:, 14 + p:15 + p], ssq[:, 12 + p:13 + p], AF.Ln, [bssq, blsm], [bssq], scale=1.0 / D, bias=eps_ap)
                act(ssq[:, 12 + p:13 + p], ssq[:, 14 + p:15 + p], AF.Exp, [bssq], [bssq], scale=-0.5)
                stt(vn[:, tk, :], vstg[p][:, :], ssq[:, 12 + p:13 + p], sgu_bc[:, :], ALU.mult, ALU.mult,
                    [bvstg[p], bssq, bsgu], [bvn_t[tk]])
                if tk >= 1:
                    spatial(tk - 1)
            spatial(5)
            cos2 = rope[:, 0:128].unsqueeze(1).to_broadcast([128, 2, 128])
            sin2 = rope[:, 128:256].unsqueeze(1).to_broadcast([128, 2, 128])

            def rope_apply(dst, bank1, bb1, bdst):
                qf = tmp[0][:, 0:256]
                acopy(qf, bank1[:, 0:256], [bb1], [btmp[0]])
                mm(bank1[:, 256:512], perm[:, :], qf, True, True, [bperm, btmp[0]], [bb1])
                tt(tmp[1][:, 0:256].rearrange("p (s t) -> p s t", s=2), qf.rearrange("p (s t) -> p s t", s=2), cos2, ALU.mult,
                   [btmp[0], brope], [btmp[1]])
                tt(tmp[1][:, 256:512].rearrange("p (s t) -> p s t", s=2), bank1[:, 256:512].rearrange("p (s t) -> p s t", s=2), sin2,
                   ALU.mult, [bb1, brope], [btmp[1]])
                tt(dst, tmp[1][:, 0:256], tmp[1][:, 256:512], ALU.add, [btmp[1]], [bdst])

            rope_pend = []
            for kind in ("mq", "mk"):
                for pi in range(2):
                    sv, bs = get(kind, l, pi)
                    for jf in range(4):
                        ft = pi * 4 + jf
                        s = ft % 3
                        b0, b1 = B[2 * s], B[2 * s + 1]
                        for kt in range(8):
                            mm(b0[:, :], sv[:, kt, jf * 128:(jf + 1) * 128], hT[:, kt, 0:512], kt == 0, kt == 7, [bs, bhT], [bB[2 * s]])
                        for kt in range(8):
                            mm(b1[:, 0:256], sv[:, kt, jf * 128:(jf + 1) * 128], hT[:, kt, 512:768], kt == 0, kt == 7, [bs, bhT], [bB[2 * s + 1]])
                        for fn_ in rope_pend:
                            fn_()
                        del rope_pend[:]
                        if kind == "mq":
                            acopy(qT[:, ft, 0:512], b0[:, :], [bB[2 * s]], [bq])
                            rope_pend.append(lambda ft=ft, b1=b1, s=s: rope_apply(qT[:, ft, 512:768], b1, bB[2 * s + 1], bq))
                        else:
                            p = ft % 2
                            acopy(kstg[p][:, :], b0[:, :], [bB[2 * s]], [bkstg[p]])
                            ld("sp", nkT_d[l, ft * 128:(ft + 1) * 128, :], kstg[p][:, :], [bout_k], bkstg[p], reads=[bkstg[p]])
                            vcopy(kp[:, ft, :], kstg[p][:, :], [bkstg[p]], [bkp])
                            rope_pend.append(lambda ft=ft, b1=b1, s=s: rope_apply(ksmp[:, ft, :], b1, bB[2 * s + 1], bksmp))
            for fn_ in rope_pend:
                fn_()
            del rope_pend[:]
            ld("sp", agk_in.rearrange("(f p) t -> p f t", p=128), ksmp[:, :, :], [bagk_in], bagk_in, reads=[bksmp])
            P.dma("pool", lambda e: e.collective_compute("AllGather", ALU.bypass, replica_groups=[list(range(NCORES))],
                                                         ins=[agk_in.opt()], outs=[agk_out.opt()]),
                  reads=[bagk_in], writes=[bagk_out], owner=bagk_out, inc=1)
            sv0, bs0 = get("mva", l, 0)
            sv1, bs1 = pull("mva", l, 1)
            rot = 0
            for tk in range(6):
                p = tk % 2
                for cb, (sv, bs) in enumerate(((sv0, bs0), (sv1, bs1))):
                    bk = rot % 6
                    rot += 1
                    for kt in range(8):
                        mm(B[bk][:, :], hT[:, kt, tk * 128:(tk + 1) * 128], sv[:, kt, :], kt == 0, kt == 7, [bs, bhT], [bB[bk]])
                    acopy(vstg[p][:, cb * 512:(cb + 1) * 512], B[bk][:, :], [bB[bk]], [bvstg[p]])
                if tk < 4:
                    ld("sp", nv_d[l, tk * 128:(tk + 1) * 128, :], vstg[p][:, :], [bout_v], bvstg[p], reads=[bvstg[p]])
                    vcopy(Vb[:, tk, :], vstg[p][:, :], [bvstg[p]], [bVb])
                else:
                    vcopy(vsmp[:, tk - 4, :], vstg[p][:, :], [bvstg[p]], [bvsmp])
            ld("sp", agv_in.rearrange("(j p) c -> p j c", p=128), vsmp[:, :, :], [bagv_in], bagv_in, reads=[bvsmp])
            P.dma("pool", lambda e: e.collective_compute("AllGather", ALU.bypass, replica_groups=[list(range(NCORES))],
                                                         ins=[agv_in.opt()], outs=[agv_out.opt()]),
                  reads=[bagv_in], writes=[bagv_out], owner=bagv_out, inc=1)
            units = [("p", a, h) for a in range(2) for h in range(8)]
            units += [("s", s_, hg, hl) for s_ in range(2) for hg in range(2) for hl in range(4)]
            NU = len(units)
            groups = [(s_, hg) for s_ in range(2) for hg in range(2)]
            st_ = {"srot": 0, "orot": 0}
            avinfo = {}

            def load_K(s_, hg):
                ld("pool", K_s[:, :, 0:256], ckT_d[l, s_, hg * 512:(hg + 1) * 512, :].rearrange("(h p) t -> p h t", p=128), [bKs], bKs)
                for r in range(NCORES):
                    ld("sp", K_s[:, :, 256 + r * 128: 256 + (r + 1) * 128],
                       agk_out[r * 1024 + hg * 512: r * 1024 + (hg + 1) * 512, s_ * 128:(s_ + 1) * 128].rearrange("(h p) t -> p h t", p=128),
                       [bKs], bKs, reads=[bagk_out])

            def load_V(s_, hg):
                ld("pool", V_s[:, 0:2, :], cv_d[l, s_, :, hg * 512:(hg + 1) * 512].rearrange("(j p) c -> p j c", p=128), [bVs], bVs)
                for r in range(NCORES):
                    ld("sp", V_s[:, 2 + r, :], agv_out[r * 256 + s_ * 128: r * 256 + (s_ + 1) * 128, hg * 512:(hg + 1) * 512],
                       [bVs], bVs, reads=[bagv_out])

            def emit_S(k):
                u = units[k]
                ETc, bETc = (ET, bET) if k % 2 == 0 else (ET2, bET2)
                if u[0] == "p":
                    _, a, h = u
                    for i in range(2):
                        sbk = st_["srot"] % 3
                        st_["srot"] += 1
                        for k2 in range(2):
                            mm(B[sbk][:, k2 * 256:(k2 + 1) * 256], kp[i * 64:(i + 1) * 64, h, a * 256 + k2 * 128: a * 256 + (k2 + 1) * 128],
                               qT[i * 64:(i + 1) * 64, h, a * 256:(a + 1) * 256], True, True, [bkp, bq], [bB[sbk]])
                        act(ETc[i][:, 0:512], B[sbk][:, :], AF.Exp, [bB[sbk]], [bETc[i]], scale=0.125)
                else:
                    _, s_, hg, hl = u
                    h = hg * 4 + hl
                    q0 = 512 + s_ * 128
                    for i in range(2):
                        for (t0, t1) in ((0, 4), (4, 8), (8, 10)):
                            sbk = st_["srot"] % 3
                            st_["srot"] += 1
                            for t in range(t0, t1):
                                mm(B[sbk][:, (t - t0) * 128:(t - t0 + 1) * 128], K_s[i * 64:(i + 1) * 64, hl, t * 128:(t + 1) * 128],
                                   qT[i * 64:(i + 1) * 64, h, q0:q0 + 128], True, True, [bKs, bq], [bB[sbk]])
                            nn = (t1 - t0) * 128
                            act(ETc[i][:, t0 * 128:t1 * 128], B[sbk][:, 0:nn], AF.Exp, [bB[sbk]], [bETc[i]], scale=0.125)
                    if hl == 3:
                        gi = groups.index((s_, hg))
                        if gi + 1 < len(groups):
                            load_K(*groups[gi + 1])

            def emit_AV(k):
                u = units[k]
                ETc, bETc = (ET, bET) if k % 2 == 0 else (ET2, bET2)
                if u[0] == "p":
                    _, a, h = u
                    os_ = st_["orot"] % 2
                    st_["orot"] += 1
                    Bo = [B[3 + 2 * os_], B[4 + 2 * os_]]
                    bBo = [bB[3 + 2 * os_], bB[4 + 2 * os_]]
                    for i in range(2):
                        for k2 in range(2):
                            mm(Bo[i][:, 0:256], Vb[:, a * 2 + k2, h * 128:(h + 1) * 128], ETc[i][:, k2 * 256:(k2 + 1) * 256],
                               k2 == 0, k2 == 1, [bVb, bETc[i]], [bBo[i]])
                        for k2 in range(2):
                            mm(Bo[i][:, 256:512], ones[:, :], ETc[i][:, k2 * 256:(k2 + 1) * 256], k2 == 0, k2 == 1, [bones, bETc[i]], [bBo[i]])
                    avinfo[k] = (h, a * 256, 256, Bo[0][:, 0:256], Bo[0][:, 256:512], Bo[1][:, 0:256], Bo[1][:, 256:512], bBo[0], bBo[1])
                else:
                    _, s_, hg, hl = u
                    h = hg * 4 + hl
                    q0 = 512 + s_ * 128
                    os_ = st_["orot"] % 2
                    st_["orot"] += 1
                    Bo = [B[3 + 2 * os_], B[4 + 2 * os_]]
                    bBo = [bB[3 + 2 * os_], bB[4 + 2 * os_]]
                    for i in range(2):
                        for t in range(10):
                            mm(Bo[i][:, 0:128], V_s[:, t, hl * 128:(hl + 1) * 128], ETc[i][:, t * 128:(t + 1) * 128],
                               t == 0, t == 9, [bVs, bETc[i]], [bBo[i]])
                        for t in range(10):
                            mm(Bo[i][:, 128:256], ones[:, :], ETc[i][:, t * 128:(t + 1) * 128],
                               t == 0, t == 9, [bones, bETc[i]], [bBo[i]])
                    avinfo[k] = (h, q0, 128, Bo[0][:, 0:128], Bo[0][:, 128:256], Bo[1][:, 0:128], Bo[1][:, 128:256], bBo[0], bBo[1])
                    if hl == 3:
                        gi = groups.index((s_, hg))
                        if gi + 1 < len(groups):
                            load_V(*groups[gi + 1])

            def post1(k):
                h, tok0, n, o1, s1, o2, s2, bo1, bo2 = avinfo[k]
                par = k % 3
                r1 = tmp[0][:, 0:n]
                l2 = tmp[2][:, 0:n]; r2 = tmp[2][:, 256:256 + n]
                t1 = tmp[1][:, 0:n]; t2 = tmp[1][:, 256:256 + n]
                oo = oo_b[par][:, 0:n]
                P.op("dve", lambda e: e.reciprocal(out=r1, in_=s1), reads=[bo1], writes=[btmp[0]])
                act(l2, s2, AF.Ln, [bo2], [btmp[2]])
                act(r2, l2, AF.Exp, [btmp[2]], [btmp[2]], scale=-1.0)
                tt(t1, o1, r1, ALU.mult, [bo1, btmp[0]], [btmp[1]])
                tt(t2, o2, r2, ALU.mult, [bo2, btmp[2]], [btmp[1]])
                stt(oo, t2, lsm[:, 16 + l:17 + l], t1, ALU.mult, ALU.add, [btmp[1], blsm], [boo[par]])

            def post_sq(k):
                h, tok0, n = avinfo[k][0:3]
                par = k % 3
                act(osq_b[par][:, 0:n], oo_b[par][:, 0:n], AF.Square, [boo[par]], [bosq_b[par]])

            def post2(k):
                h, tok0, n = avinfo[k][0:3]
                par = k % 3
                hb = (k % 2) * 256
                oo = oo_b[par][:, 0:n]
                lt = tmp[3][:, 0:n]; rs = tmp[3][:, 256:256 + n]
                mm(B[7][:, hb:hb + n], ones[:, :], osq_b[par][:, 0:n], True, True, [bones, bosq_b[par]], [bB[7]])
                act(lt, B[7][:, hb:hb + n], AF.Ln, [bB[7], blsm], [btmp[3]], scale=1.0 / 128, bias=eps_ap)
                act(rs, lt, AF.Exp, [btmp[3]], [btmp[3]], scale=-0.5)
                stt(bT[:, h, tok0:tok0 + n], oo, lsm[:, 20 + l:21 + l], rs, ALU.mult, ALU.mult, [boo[par], btmp[3], blsm], [bbT])

            load_K(*groups[0])
            load_V(*groups[0])
            emit_S(0)
            for k in range(NU):
                if k + 1 < NU:
                    emit_S(k + 1)
                emit_AV(k)
                post1(k)
                if k >= 1:
                    post_sq(k - 1)
                if k >= 2:
                    post2(k - 2)
            post_sq(NU - 1)
            post2(NU - 2)
            post2(NU - 1)
            rot = 0
            for f in range(8):
                sv, bs = get("mb", l, f)
                for cq in range(3):
                    c0, c1 = cq * 256, (cq + 1) * 256
                    st = rot % 3
                    rot += 1
                    Bg_, Bab = B[2 * st], B[2 * st + 1]
                    bg_, bab = bB[2 * st], bB[2 * st + 1]
                    for kt in range(8):
                        mm(Bg_[:, 0:256], sv[:, kt, 0:128], hT[:, kt, c0:c1], kt == 0, kt == 7, [bs, bhT], [bg_])
                    for kt in range(8):
                        mm(Bg_[:, 256:512], sv[:, kt, 128:256], hT[:, kt, c0:c1], kt == 0, kt == 7, [bs, bhT], [bg_])
                    for kt in range(8):
                        mm(Bab[:, 0:256], sv[:, kt, 256:384], uaT[:, kt, c0:c1], kt == 0, kt == 7, [bs, bua], [bab])
                    for kt in range(8):
                        mm(Bab[:, 256:512], sv[:, kt, 384:512], bT[:, kt, c0:c1], kt == 0, kt == 7, [bs, bbT], [bab])
                    ta, tb = (rot % 2) * 2, (rot % 2) * 2 + 1
                    act(tmp[ta][:, :], Bg_[:, :], AF.Sigmoid, [bg_], [btmp[ta]])
                    tt(tmp[tb][:, :], tmp[ta][:, :], Bab[:, :], ALU.mult, [btmp[ta], bab], [btmp[tb]])
                    tt(mT[:, f, c0:c1], tmp[tb][:, 0:256], tmp[tb][:, 256:512], ALU.add, [btmp[tb]], [bq])
            for pi in range(2):
                sv, bs = get("mo", l, pi)
                for jf in range(4):
                    ft = pi * 4 + jf
                    s = ft % 3
                    for kt in range(8):
                        mm(B[2 * s][:, :], sv[:, kt, jf * 128:(jf + 1) * 128], mT[:, kt, 0:512], kt == 0, kt == 7, [bs, bq], [bB[2 * s]])
                    for kt in range(8):
                        mm(B[2 * s + 1][:, 0:256], sv[:, kt, jf * 128:(jf + 1) * 128], mT[:, kt, 512:768], kt == 0, kt == 7, [bs, bq], [bB[2 * s + 1]])
                    residual(l, 1, ft, B[2 * s], bB[2 * s], B[2 * s + 1], bB[2 * s + 1])

        ld("sp", condT[:, :], condT_d, [bcond], bcond)
        ld("sp", consts[:, :], consts_d, [bconsts], bconsts)
        ld("sp", lam_bc[:, :], lam_d.partition_broadcast(128), [blam], blam)
        ld("sp", rope[:, :], rope_d, [brope], brope)
        ld("sp", perm[:, :], perm_d, [bperm], bperm)
        ld("sp", xT[:, :, :], xT_d.rearrange("(f p) t -> p f t", p=128), bxT, bsetup)
        ld("pool", wsT[:, :], wsT_d, [bws], bws)
        P.op("dve", lambda e: e.memset(ones[:, :], 1.0), writes=[bones])
        P.op("dve", lambda e: e.memset(lsm[:, 24:25], EPS), writes=[blsm])
        act(scT[:, :], condT[:, :], AF.Silu, [bcond], [bsc])
        lv = lam_bc[:, :].rearrange("p (l a b d) -> p l a b d", l=4, a=2, b=2)
        tt(tmp[0][:, :].rearrange("p (l a d) -> p l a d", l=4, a=2), lv[:, :, :, 0, :], lv[:, :, :, 1, :], ALU.mult, [blam], [btmp[0]])
        P.op("dve", lambda e: e.tensor_reduce(out=lsm[:, 0:8], in_=tmp[0][:, :].rearrange("p (k d) -> p k d", d=64), axis=AX.X, op=ALU.add),
             reads=[btmp[0]], writes=[blsm])
        act(lsm[:, 8:16], lsm[:, 0:8], AF.Exp, [blsm], [blsm])
        le = lsm[:, 8:16].rearrange("p (l a) -> p l a", a=2)
        tt(lsm[:, 16:20], le[:, :, 1], le[:, :, 0], ALU.subtract, [blsm], [blsm])
        for l in range(4):
            ts(lsm[:, 16 + l:17 + l], lsm[:, 16 + l:17 + l], -lam_init(l), None, ALU.add, None, [blsm], [blsm])
            ts(lsm[:, 20 + l:21 + l], consts[:, l * CL + 24:l * CL + 25], 1.0 - lam_init(l), None, ALU.mult, None, [bconsts, blsm], [blsm])
        ld("sp", bmr[:, :], bmod_r, [bbmr], bbmr)
        mod_setup()
        for l in range(NL):
            norm(l, 0)
            ffn(l, 0)
            norm(l, 1)
            mixer(l)
            norm(l, 2)
            ffn(l, 1)
        norm_stats()
        gf = 4 * CL
        for ft in range(8):
            p = ft % 2
            stt(nt[p][:, :], xT[:, ft, :], consts[:, gf + ft:gf + ft + 1], rstd[:, :], ALU.mult, ALU.mult,
                [bxT[ft], bconsts, brstd], [bnt[p]])
            ld("sp", yT_d[ft * 128:(ft + 1) * 128, :], nt[p][:, :], [bout_y], bnt[p], reads=[bnt[p]])
        P.wait_all("sp", [bout_y, bout_k, bout_v])
        assert pstate["next_use"] == len(plist), (pstate, len(plist))
        P.emit()
    return nc


def _rope_tables(r):
    pos = r * 128 + np.arange(128)
    row = (pos // 64).astype(np.float32)
    col = (pos % 64).astype(np.float32)
    freqs = (np.float32(10000.0) ** (-np.arange(0, 32, 2, dtype=np.float32) / np.float32(32))).astype(np.float32)
    cos = np.zeros((128, 128), np.float32)
    sin = np.zeros((128, 128), np.float32)
    for p in range(128):
        d = p % 64
        jj = d % 32
        base = row if d < 32 else col
        ang = (base * freqs[jj % 16]).astype(np.float32)
        cos[p] = np.cos(ang)
        sin[p] = np.sin(ang) * (-1.0 if jj < 16 else 1.0)
    return np.concatenate([cos, sin], axis=1).astype(np.float32)


def _perm_matrix():
    m = np.zeros((128, 128), np.float32)
    for p in range(128):
        jj = (p % 64) % 32
        partner = p + 16 if jj < 16 else p - 16
        m[partner, p] = 1.0
    return m


_NC_CACHE = {}


def kernel(x_prompt, x_sample, cache_k, cache_v, c, c_ctx, w_mod, b_mod, g_norm, ffn1_w13, ffn1_w2, w_in, sgu_gain,
           w_spatial, b_spatial, lam, subln_gain, w_branch_a, w_branch_b, w_out, ffn2_w13, ffn2_w2, g_final, _NL=4):
    f = lambda a: np.ascontiguousarray(np.asarray(a, dtype=np.float32))
    x_prompt, x_sample, cache_k, cache_v = f(x_prompt), f(x_sample), f(cache_k), f(cache_v)
    NL = _NL
    if NL not in _NC_CACHE:
        _NC_CACHE[NL] = build(NL)
    nc = _NC_CACHE[NL]
    conds = np.stack([f(c_ctx), f(c)[0], f(c)[1]], axis=1)
    condT = np.ascontiguousarray(conds.reshape(8, 128, 3).transpose(1, 0, 2).reshape(128, 24))
    consts = np.zeros((128, NCONST), np.float32)
    for l in range(4):
        consts[:, l * CL:l * CL + 24] = f(g_norm)[l].reshape(24, 128).T
        consts[:, l * CL + 24] = f(subln_gain)[l]
    consts[:, 4 * CL:4 * CL + 8] = f(g_final).reshape(8, 128).T
    wsT = np.ascontiguousarray(f(w_spatial).transpose(3, 0, 1, 2).reshape(128, 2048))
    ckT = np.ascontiguousarray(cache_k.reshape(2, 4, 256, 1024).transpose(1, 0, 3, 2))
    cv = np.ascontiguousarray(cache_v.reshape(2, 4, 256, 1024).transpose(1, 0, 2, 3))
    shared = {
        "condT": condT, "consts": consts, "sgu": f(sgu_gain), "bsp": f(b_spatial).reshape(4, 512),
        "lam": f(lam).reshape(1, 1024), "wsT": wsT, "perm": _perm_matrix(),
        "ffn1_w13": f(ffn1_w13), "ffn1_w2": f(ffn1_w2), "w_in": f(w_in),
        "w_branch_a": f(w_branch_a), "w_branch_b": f(w_branch_b), "w_out": f(w_out),
        "ffn2_w13": f(ffn2_w13), "ffn2_w2": f(ffn2_w2), "ckT": ckT, "cv": cv,
    }
    in_maps = []
    for r in range(NCORES):
        xc = np.concatenate([x_prompt[2 * r], x_prompt[2 * r + 1], x_sample[0, r * 128:(r + 1) * 128],
                             x_sample[1, r * 128:(r + 1) * 128]], axis=0)
        m = dict(shared)
        m["xT"] = np.ascontiguousarray(xc.T)
        m["rope"] = _rope_tables(r)
        m["wmod_r"] = np.ascontiguousarray(f(w_mod)[r // 2][:, (r % 2) * 4608:(r % 2 + 1) * 4608])
        m["bmod_r"] = np.ascontiguousarray(f(b_mod)[r // 2][(r % 2) * 4608:(r % 2 + 1) * 4608].reshape(36, 128).T)
        in_maps.append(m)
    res = run_bass_kernel_spmd(nc, in_maps, core_ids=list(range(NCORES)))
    y_prompt = np.zeros((16, 256, 1024), np.float32)
    y_sample = np.zeros((2, 1024, 1024), np.float32)
    nk = np.zeros((16, NL, 256, 8, 128), np.float32)
    nv = np.zeros((16, NL, 256, 8, 128), np.float32)
    for r in range(NCORES):
        o = res.results[r]
        y = np.asarray(o["yT"]).T
        y_prompt[2 * r] = y[0:256]
        y_prompt[2 * r + 1] = y[256:512]
        y_sample[0, r * 128:(r + 1) * 128] = y[512:640]
        y_sample[1, r * 128:(r + 1) * 128] = y[640:768]
        k_ = np.asarray(o["nkT"])
        v_ = np.asarray(o["nv"])
        for l in range(NL):
            kt = k_[l].T
            for a in range(2):
                nk[2 * r + a, l] = kt[a * 256:(a + 1) * 256].reshape(256, 8, 128)
                nv[2 * r + a, l] = v_[l][a * 256:(a + 1) * 256].reshape(256, 8, 128)
    return (y_prompt, y_sample, nk, nv)
```
